# Optimizing a Trainium2 kernel written in Bass

```python
import math
import jax, jax.numpy as jnp
from jax import lax
import numpy as np

D_MODEL = 2048
BATCH = 4
SEQ = 4096
DEPTH = 4
DEC_BATCH = 8
DEC_SEQ = 32
PAST_LEN = 1024

CHUNK = 64
QBLOCK = 128
N_MIXERS = 3
D_INNER = D_MODEL
NORM_EPS = 1e-6
GLA_HEADS = 4
GLA_KD = D_INNER // 2
GLA_DK = GLA_KD // GLA_HEADS
GLA_DV = D_INNER // GLA_HEADS
GLA_GATE_RANK = 16
GLA_GATE_NORMALIZER = 16.0
HGRN_EXPAND = 128
HGRN_HEADS = D_INNER // HGRN_EXPAND
HGRN_DK = HGRN_EXPAND
HGRN_DV = D_INNER // HGRN_HEADS
DIFF_HD = 128
DIFF_HEADS = D_INNER // (2 * DIFF_HD)
N_GLA = (DEPTH + 2) // 3
N_HGRN = (DEPTH + 1) // 3
N_DIFF = DEPTH // 3

kernel_name = 'hybrid_gla_hgrn2_diffattn_stream_step'


def _rmsnorm(x, w):
    xf = x.astype(jnp.float32)
    y = xf * lax.rsqrt(jnp.mean(xf * xf, axis=-1, keepdims=True) + NORM_EPS)
    return (y * w.astype(jnp.float32)).astype(x.dtype)


def _gated_linear_chunk(S, q, k, v, g):
    c = q.shape[2]
    b = jnp.cumsum(g, axis=2)
    o_inter = jnp.einsum('bhck,bhkv->bhcv', q * jnp.exp(b), S)
    causal = jnp.tril(jnp.ones((c, c), dtype=bool))[:, :, None]
    rel = jnp.where(causal, b[:, :, :, None, :] - b[:, :, None, :, :], -jnp.inf)
    attn = jnp.einsum('bhtk,bhsk,bhtsk->bhts', q, k, jnp.exp(rel))
    o_intra = jnp.einsum('bhts,bhsv->bhtv', attn, v)
    b_last = b[:, :, -1, :]
    k_dec = k * jnp.exp(b_last[:, :, None, :] - b)
    S_new = jnp.exp(b_last)[..., None] * S + jnp.einsum('bhck,bhcv->bhkv', k_dec, v)
    return S_new, o_inter + o_intra


def _gated_linear_recurrence(q, k, v, g, S0):
    B, T, H, _ = q.shape
    c = min(CHUNK, T)
    n = T // c

    def to_chunks(a):
        return a.astype(jnp.float32).reshape(B, n, c, H, a.shape[-1]).transpose(1, 0, 3, 2, 4)

    S, o = lax.scan(lambda s, inp: _gated_linear_chunk(s, *inp), S0.astype(jnp.float32),
                    (to_chunks(q), to_chunks(k), to_chunks(v), to_chunks(g)))
    o = o.transpose(1, 0, 3, 2, 4).reshape(B, T, H, v.shape[-1])
    return o, S


def _gla_branch(h, S0, w_in, w_a1, w_a2, b_a, norm_w, w_out):
    B, T, _ = h.shape
    q, k, v, z = jnp.split(h @ w_in, [GLA_KD, 2 * GLA_KD, 2 * GLA_KD + D_INNER], axis=-1)
    glog = jax.nn.log_sigmoid(((h @ w_a1) @ w_a2 + b_a).astype(jnp.float32)) / GLA_GATE_NORMALIZER
    q = q.reshape(B, T, GLA_HEADS, GLA_DK) * (GLA_DK ** -0.5)
    k = k.reshape(B, T, GLA_HEADS, GLA_DK)
    v = v.reshape(B, T, GLA_HEADS, GLA_DV)
    glog = glog.reshape(B, T, GLA_HEADS, GLA_DK)
    o, S = _gated_linear_recurrence(q, k, v, glog, S0)
    o = _rmsnorm(o, norm_w).reshape(B, T, D_INNER).astype(h.dtype)
    return (o * jax.nn.silu(z)) @ w_out, S


def _hgrn2_branch(h, S0, lower_bound, w_in, norm_w, w_out):
    B, T, _ = h.shape
    q, f, i, z = jnp.split(h @ w_in, 4, axis=-1)
    q = jax.nn.silu(q)
    fgate = lower_bound + (1.0 - lower_bound) * jax.nn.sigmoid(f.astype(jnp.float32))
    k = 1.0 - fgate
    g = jnp.log(fgate)
    q = q.reshape(B, T, HGRN_HEADS, HGRN_DK) * (HGRN_DK ** -0.5)
    k = k.reshape(B, T, HGRN_HEADS, HGRN_DK)
    g = g.reshape(B, T, HGRN_HEADS, HGRN_DK)
    i = i.reshape(B, T, HGRN_HEADS, HGRN_DV)
    o, S = _gated_linear_recurrence(q, k, i, g, S0)
    o = _rmsnorm(o.reshape(B, T, D_INNER), norm_w).astype(h.dtype)
    return (o * jax.nn.silu(z)) @ w_out, S


def _diff_project(h, w_in):
    B, T, _ = h.shape
    q, k, v, z = jnp.split(h @ w_in, 4, axis=-1)
    q = q.reshape(B, T, 2 * DIFF_HEADS, DIFF_HD)
    k = k.reshape(B, T, 2 * DIFF_HEADS, DIFF_HD)
    v = v.reshape(B, T, DIFF_HEADS, 2 * DIFF_HD)
    return q, k, v, z


def _diff_lambda(lam_p, lam_init):
    lp = lam_p.astype(jnp.float32)
    return jnp.exp(jnp.sum(lp[0] * lp[1])) - jnp.exp(jnp.sum(lp[2] * lp[3])) + lam_init


def _diff_attend(q, k, v, lam, mask):
    B, Tq = q.shape[:2]
    Tk = k.shape[1]
    s = jnp.einsum('bqnd,bknd->bnqk', q, k, preferred_element_type=jnp.float32) * (DIFF_HD ** -0.5)
    if mask is not None:
        s = jnp.where(mask, s, -jnp.inf)
    p = jax.nn.softmax(s, axis=-1).reshape(B, DIFF_HEADS, 2, Tq, Tk)
    a = p[:, :, 0] - lam * p[:, :, 1]
    return jnp.einsum('bhqk,bkhe->bqhe', a, v.astype(jnp.float32))


def _diff_attend_prompt(q, k, v, lam):
    B, T = q.shape[:2]
    nb = T // QBLOCK
    qb = q.reshape(B, nb, QBLOCK, 2 * DIFF_HEADS, DIFF_HD).swapaxes(0, 1)
    key_chunk = jnp.arange(T) // CHUNK

    def block(args):
        qblk, start = args
        q_chunk = (start + jnp.arange(QBLOCK)) // CHUNK
        mask = key_chunk[None, :] <= q_chunk[:, None]
        return _diff_attend(qblk, k, v, lam, mask)

    o = lax.map(block, (qb, jnp.arange(nb) * QBLOCK))
    return o.swapaxes(0, 1).reshape(B, T, DIFF_HEADS, 2 * DIFF_HD)


def _diff_output(o, z, lam_init, subln_w, w_out):
    B, T = o.shape[:2]
    o = _rmsnorm(o, subln_w) * (1.0 - lam_init)
    o = o.reshape(B, T, D_INNER).astype(z.dtype)
    return (o * jax.nn.silu(z)) @ w_out


def setup_inputs(seed: int = 0) -> dict:
    key = jax.random.key(seed)
    ks = jax.random.split(key, 24)

    def nrm(k, shape, scale=1.0):
        return jax.random.normal(k, shape, jnp.float32) * scale

    gla_cols = 2 * GLA_KD + 2 * D_INNER
    return {
        'x_prompt': nrm(ks[0], (BATCH, SEQ, D_MODEL)),
        'x_sample': nrm(ks[1], (DEC_BATCH, DEC_SEQ, D_MODEL)),
        'state_gla': nrm(ks[2], (N_GLA, DEC_BATCH, GLA_HEADS, GLA_DK, GLA_DV)),
        'state_hgrn': nrm(ks[3], (N_HGRN, DEC_BATCH, HGRN_HEADS, HGRN_DK, HGRN_DV), 0.5),
        'cache_k': nrm(ks[4], (N_DIFF, DEC_BATCH, PAST_LEN, 2 * DIFF_HEADS, DIFF_HD)),
        'cache_v': nrm(ks[5], (N_DIFF, DEC_BATCH, PAST_LEN, DIFF_HEADS, 2 * DIFF_HD)),
        'norm_w': 1.0 + nrm(ks[6], (DEPTH, D_MODEL), 0.02),
        'final_norm_w': 1.0 + nrm(ks[7], (D_MODEL,), 0.02),
        'gla_w_in': nrm(ks[8], (N_GLA, D_MODEL, gla_cols), D_MODEL ** -0.5),
        'gla_w_a1': nrm(ks[9], (N_GLA, D_MODEL, GLA_GATE_RANK), D_MODEL ** -0.5),
        'gla_w_a2': nrm(ks[10], (N_GLA, GLA_GATE_RANK, GLA_KD), GLA_GATE_RANK ** -0.5),
        'gla_b_a': nrm(ks[11], (N_GLA, GLA_KD), 0.1),
        'gla_norm_w': 1.0 + nrm(ks[12], (N_GLA, GLA_DV), 0.02),
        'gla_w_out': nrm(ks[13], (N_GLA, D_INNER, D_MODEL), D_INNER ** -0.5),
        'hgrn_w_in': nrm(ks[14], (N_HGRN, D_MODEL, 4 * D_INNER), D_MODEL ** -0.5),
        'hgrn_lower_bounds': nrm(ks[15], (DEPTH, D_INNER), 0.1),
        'hgrn_norm_w': 1.0 + nrm(ks[16], (N_HGRN, D_INNER), 0.02),
        'hgrn_w_out': nrm(ks[17], (N_HGRN, D_INNER, D_MODEL), D_INNER ** -0.5),
        'diff_w_in': nrm(ks[18], (N_DIFF, D_MODEL, 4 * D_INNER), D_MODEL ** -0.5),
        'diff_lambda': nrm(ks[19], (N_DIFF, 4, DIFF_HD), 0.1),
        'diff_subln_w': 1.0 + nrm(ks[20], (N_DIFF, 2 * DIFF_HD), 0.02),
        'diff_w_out': nrm(ks[21], (N_DIFF, D_INNER, D_MODEL), D_INNER ** -0.5),
    }


def reference(x_prompt, x_sample, state_gla, state_hgrn, cache_k, cache_v,
              norm_w, final_norm_w,
              gla_w_in, gla_w_a1, gla_w_a2, gla_b_a, gla_norm_w, gla_w_out,
              hgrn_w_in, hgrn_lower_bounds, hgrn_norm_w, hgrn_w_out,
              diff_w_in, diff_lambda, diff_subln_w, diff_w_out):
    lb = jax.nn.softmax(hgrn_lower_bounds.astype(jnp.float32), axis=0)
    lb = jnp.cumsum(lb, axis=0) - lb[0]

    xp, xs = x_prompt, x_sample
    Bp = xp.shape[0]
    gla_p, gla_s, hg_p, hg_s, kp_l, vp_l, ks_l, vs_l = [], [], [], [], [], [], [], []
    ia = ib = ic = 0
    for i in range(DEPTH):
        hp = _rmsnorm(xp, norm_w[i])
        hs = _rmsnorm(xs, norm_w[i])
        kind = i % N_MIXERS
        if kind == 0:
            j = ia
            ia += 1
            w = (gla_w_in[j], gla_w_a1[j], gla_w_a2[j], gla_b_a[j], gla_norm_w[j], gla_w_out[j])
            S0 = jnp.zeros((Bp, GLA_HEADS, GLA_DK, GLA_DV), jnp.float32)
            yp, Sp = _gla_branch(hp, S0, *w)
            ys, Ss = _gla_branch(hs, state_gla[j], *w)
            gla_p.append(Sp)
            gla_s.append(Ss)
        elif kind == 1:
            j = ib
            ib += 1
            w = (hgrn_w_in[j], hgrn_norm_w[j], hgrn_w_out[j])
            S0 = jnp.zeros((Bp, HGRN_HEADS, HGRN_DK, HGRN_DV), jnp.float32)
            yp, Sp = _hgrn2_branch(hp, S0, lb[i], *w)
            ys, Ss = _hgrn2_branch(hs, state_hgrn[j], lb[i], *w)
            hg_p.append(Sp)
            hg_s.append(Ss)
        else:
            j = ic
            ic += 1
            lam_init = 0.8 - 0.6 * math.exp(-0.3 * i)
            lam = _diff_lambda(diff_lambda[j], lam_init)
            qp, kp, vp, zp = _diff_project(hp, diff_w_in[j])
            op = _diff_attend_prompt(qp, kp, vp, lam)
            yp = _diff_output(op, zp, lam_init, diff_subln_w[j], diff_w_out[j])
            qs, kn, vn, zs = _diff_project(hs, diff_w_in[j])
            k_all = jnp.concatenate([cache_k[j].astype(kn.dtype), kn], axis=1)
            v_all = jnp.concatenate([cache_v[j].astype(vn.dtype), vn], axis=1)
            os_ = _diff_attend(qs, k_all, v_all, lam, None)
            ys = _diff_output(os_, zs, lam_init, diff_subln_w[j], diff_w_out[j])
            kp_l.append(kp)
            vp_l.append(vp)
            ks_l.append(kn)
            vs_l.append(vn)
        xp = xp + yp.astype(xp.dtype)
        xs = xs + ys.astype(xs.dtype)

    y_prompt = _rmsnorm(xp, final_norm_w)
    y_sample = _rmsnorm(xs, final_norm_w)
    gla_state_p = jnp.stack(gla_p)
    gla_state_s = jnp.stack(gla_s)
    hgrn_state_p = jnp.stack(hg_p)
    hgrn_state_s = jnp.stack(hg_s)
    k_rows_p = jnp.stack(kp_l)
    v_rows_p = jnp.stack(vp_l)
    k_rows_s = jnp.stack(ks_l)
    v_rows_s = jnp.stack(vs_l)
    return (y_prompt, y_sample, gla_state_p, gla_state_s, hgrn_state_p, hgrn_state_s,
            k_rows_p, v_rows_p, k_rows_s, v_rows_s)
```

```python
import itertools
import math
from contextlib import ExitStack

import numpy as np
import concourse.bass as bass
import concourse.mybir as mybir
from concourse.bass_utils import run_bass_kernel_spmd

F32 = mybir.dt.float32
BF16 = mybir.dt.bfloat16
AF = mybir.ActivationFunctionType
ALU = mybir.AluOpType
AX = mybir.AxisListType

DBG = ""
LIMIT = None
TRACE_RANGE = None
D = 2048
KC = 16
EPS = 1e-6
PAST = 1024
DEC = 32


class Buf:
    __slots__ = ("name", "w", "r", "excl")

    def __init__(self, name="", excl=False):
        self.name = name
        self.w = None
        self.r = {}
        self.excl = excl


class Tile:
    __slots__ = ("ap", "buf")

    def __init__(self, ap, buf=None, name=""):
        self.ap = ap
        self.buf = buf if buf is not None else Buf(name)

    def __getitem__(self, key):
        return Tile(self.ap[key], self.buf)

    def re(self, pat, **kw):
        return Tile(self.ap.rearrange(pat, **kw), self.buf)

    def bc(self, shape):
        return Tile(self.ap.to_broadcast(list(shape)), self.buf)

    def unsq(self, axis):
        return Tile(self.ap.unsqueeze(axis), self.buf)

    def sub(self, key, name=""):
        return Tile(self.ap[key], Buf(name))


def _ap(x):
    return x.ap if isinstance(x, Tile) else x


class Sched:
    def __init__(self, nc, es, n_dma=32, same_sync=True):
        self.nc = nc
        self.eng = {"pe": nc.tensor, "act": nc.scalar, "dve": nc.vector,
                    "pool": nc.gpsimd, "sp": nc.sync}
        self.sem = {k: es.enter_context(nc.semaphore("sem_" + k)) for k in self.eng}
        self.cnt = {k: 0 for k in self.eng}
        self.seen = {k: {} for k in self.eng}
        self.dsem = [es.enter_context(nc.semaphore("dsem%d" % i)) for i in range(n_dma)]
        self.dcnt = [0] * n_dma
        self.n_sw = 3
        self.dnext = self.n_sw
        self.dnext_sw = 0
        self.same_sync = same_sync
        self.n_inst = 0
        self.n_wait = 0
        self.pe_open = False

    def _handle(self, key):
        return self.sem[key[1]] if key[0] == "e" else self.dsem[key[1]]

    def _wait(self, engname, deps):
        need = {}
        for key, val in deps:
            if key[0] == "e" and key[1] == engname:
                if engname in ("pe", "sp") or not self.same_sync:
                    continue
            if need.get(key, 0) < val:
                need[key] = val
        seen = self.seen[engname]
        for key, val in need.items():
            if seen.get(key, 0) >= val:
                continue
            if DBG and 395 <= self.n_inst <= 404:
                print("  WAIT at inst", self.n_inst, engname, key, val, "cnt", dict(self.cnt), flush=True)
            self.eng[engname].wait_ge(self._handle(key), val)
            self.n_wait += 1
            seen[key] = val

    @staticmethod
    def _deps(reads, writes):
        deps = []
        for b in reads:
            if b.w is not None:
                deps.append(b.w)
        for b in writes:
            if b.w is not None:
                deps.append(b.w)
            deps.extend(b.r.items())
        return deps

    @staticmethod
    def _commit(key, val, reads, writes):
        for b in reads:
            if b.r.get(key, 0) < val:
                b.r[key] = val
        for b in writes:
            b.w = (key, val)
            b.r = {}

    @staticmethod
    def _bufs(xs):
        out = []
        for x in xs:
            if x is None or isinstance(x, (int, float)):
                continue
            b = x.buf if isinstance(x, Tile) else x
            if b not in out:
                out.append(b)
        return out

    def op(self, engname, emit, reads, writes, inc=True):
        if LIMIT is not None and self.n_inst >= LIMIT:
            if engname == "pe" and not inc:
                return
            if engname == "pe" and self.pe_open:
                pass
            else:
                return
        if engname == "pe":
            self.pe_open = not inc
        reads = self._bufs(reads)
        writes = self._bufs(writes)
        if engname != "pe":
            xr = [b for b in reads if b.excl]
            if xr:
                reads = [b for b in reads if not b.excl]
                writes = writes + [b for b in xr if b not in writes]
        self._wait(engname, self._deps(reads, writes))
        if TRACE_RANGE and TRACE_RANGE[0] <= self.n_inst < TRACE_RANGE[1]:
            import traceback
            fr = traceback.extract_stack(limit=4)
            print("  INST", self.n_inst, engname, [b.name for b in reads], "->", [b.name for b in writes],
                  "lines", [f.lineno for f in fr[:-1]], flush=True)
        ins = emit(self.eng[engname])
        self.n_inst += 1
        if inc:
            self.cnt[engname] += 1
            ins.then_inc(self.sem[engname], 1)
            self._commit(("e", engname), self.cnt[engname], reads, writes)
        else:
            self._commit(("e", engname), self.cnt[engname] + 1, reads, writes)

    def dma(self, qname, out, in_, extra_reads=(), extra_writes=(), slow=False):
        if LIMIT is not None and self.n_inst >= LIMIT:
            return
        if qname == "pool":
            j = self.dnext_sw
            self.dnext_sw = (j + 1) % self.n_sw
        else:
            j = self.dnext
            self.dnext = j + 1 if j + 1 < len(self.dsem) else self.n_sw
        reads = self._bufs([in_] + list(extra_reads))
        writes = self._bufs([out] + list(extra_writes))
        deps = self._deps(reads, writes)
        if self.dcnt[j] > 0:
            deps.append((("d", j), self.dcnt[j]))
        self._wait(qname, deps)
        kw = {"allow_slow_non_contiguous": True} if slow else {}
        self.eng[qname].dma_start(out=_ap(out), in_=_ap(in_), **kw).then_inc(self.dsem[j], 16)
        self.n_inst += 1
        self.dcnt[j] += 16
        self._commit(("d", j), self.dcnt[j], reads, writes)

    def barrier(self):
        for e in self.eng:
            deps = [(("e", k), v) for k, v in self.cnt.items() if v > 0]
            deps += [(("d", j), v) for j, v in enumerate(self.dcnt) if v > 0]
            self._wait(e, deps)

    def finish(self):
        deps = [(("d", j), v) for j, v in enumerate(self.dcnt) if v > 0]
        deps += [(("e", k), v) for k, v in self.cnt.items() if v > 0 and k != "sp"]
        self._wait("sp", deps)

    def mm(self, out, lhsT, rhs, start, stop, last=None):
        if last is None:
            last = stop
        self.op("pe", lambda e: e.matmul(_ap(out), _ap(lhsT), _ap(rhs), start=start, stop=stop),
                [lhsT, rhs], [out], inc=last)

    def tr(self, out, in_, ident, last=True):
        self.op("pe", lambda e: e.transpose(_ap(out), _ap(in_), _ap(ident)),
                [in_, ident], [out], inc=last)

    def act(self, out, in_, func, bias=None, scale=1.0, accum=None, eng="act"):
        kw = {}
        if bias is not None:
            kw["bias"] = _ap(bias)
        if accum is not None:
            kw["accum_out"] = _ap(accum)
        self.op("act", lambda e: e.activation(out=_ap(out), in_=_ap(in_), func=func,
                                              scale=_ap(scale), **kw),
                [in_, bias, scale], [out, accum])

    def tt(self, eng, out, in0, in1, op):
        self.op(eng, lambda e: e.tensor_tensor(out=_ap(out), in0=_ap(in0), in1=_ap(in1), op=op),
                [in0, in1], [out])

    def ts(self, eng, out, in0, s1, op0, s2=None, op1=None):
        if s2 is None:
            self.op(eng, lambda e: e.tensor_scalar(out=_ap(out), in0=_ap(in0), scalar1=_ap(s1),
                                                   scalar2=None, op0=op0),
                    [in0, s1], [out])
        else:
            self.op(eng, lambda e: e.tensor_scalar(out=_ap(out), in0=_ap(in0), scalar1=_ap(s1),
                                                   scalar2=_ap(s2), op0=op0, op1=op1),
                    [in0, s1, s2], [out])

    def stt(self, eng, out, in0, scalar, in1, op0, op1):
        self.op(eng, lambda e: e.scalar_tensor_tensor(out=_ap(out), in0=_ap(in0), scalar=_ap(scalar),
                                                      in1=_ap(in1), op0=op0, op1=op1),
                [in0, scalar, in1], [out])

    def copy(self, eng, out, in_):
        if eng == "act":
            self.op("act", lambda e: e.copy(out=_ap(out), in_=_ap(in_)), [in_], [out])
        else:
            self.op(eng, lambda e: e.tensor_copy(out=_ap(out), in_=_ap(in_)), [in_], [out])

    def scan(self, out, d0, d1, init, op0, op1):
        self.op("dve", lambda e: e.tensor_tensor_scan(out=_ap(out), data0=_ap(d0), data1=_ap(d1),
                                                      initial=init, op0=op0, op1=op1),
                [d0, d1], [out])

    def memset(self, eng, out, val):
        self.op(eng, lambda e: e.memset(_ap(out), val), [], [out])

    def recip(self, out, in_):
        self.op("dve", lambda e: e.reciprocal(out=_ap(out), in_=_ap(in_)), [in_], [out])

    def rmax(self, out, in_):
        self.op("dve", lambda e: e.reduce_max(out=_ap(out), in_=_ap(in_), axis=AX.X), [in_], [out])


class Ring:
    def __init__(self, tiles):
        self.tiles = tiles
        self.i = 0

    def get(self):
        t = self.tiles[self.i]
        self.i = (self.i + 1) % len(self.tiles)
        return t


class Prog:
    def __init__(self, T=4096, layers=(0, 1, 2, 3), TT=256, with_sample=True):
        self.T = T
        self.layers = tuple(layers)
        self.TT = TT
        self.NS = TT // 128
        self.with_sample = with_sample
        self.nc = bass.Bass("TRN2", target_bir_lowering=False)
        self.ncopy = 0

    def din(self, name, shape, dt=F32):
        return self.nc.dram_tensor(name, list(shape), dt, kind="ExternalInput").ap()

    def dout(self, name, shape, dt=F32):
        return self.nc.dram_tensor(name, list(shape), dt, kind="ExternalOutput").ap()

    def dint(self, name, shape, dt):
        return self.nc.dram_tensor(name, list(shape), dt, kind="Internal").ap()

    def sb(self, es, name, shape, dt):
        t = es.enter_context(self.nc.sbuf_tensor(name, list(shape), dt))
        return Tile(t[:] if len(shape) == 2 else t[(slice(None),) * len(shape)], name=name)

    def ps(self, es, name, shape, dt):
        t = es.enter_context(self.nc.psum_tensor(name, list(shape), dt))
        return Tile(t[(slice(None),) * len(shape)], buf=Buf(name, excl=True))

    def load_fm(self, es, name, src_rows, n, key, ptile):
        S = self.S
        tmp = self.sb(es, name + "_r", [n, 128], F32)
        S.dma("sp", tmp, Tile(src_rows, self.dbuf(key)))
        S.tr(ptile[:, 0:n], tmp, self.ident_f[0:n, 0:n])
        out = self.sb(es, name, [128, n], F32)
        S.copy("dve", out, ptile[:, 0:n])
        return out

    def evac(self, out, in_):
        self.ncopy += 1
        self.S.copy("act" if self.ncopy % 2 else "dve", out, in_)

    def build(self):
        nc = self.nc
        T = self.T
        with ExitStack() as es:
            self.S = S = Sched(nc, es)
            self._declare_io()
            self._consts(es)
            jobs0 = self._prep_jobs(self.layers[0])
            if self.layers[0] in (0, 3):
                first_use = []
                for g in range(4):
                    first_use += [g, 4 + g, 8 + 2 * g, 9 + 2 * g, 16 + 2 * g, 17 + 2 * g]
                jobs0 = [jobs0[i] for i in first_use] + jobs0[24:]
            for job in jobs0:
                job()
            self.xs_t = self.sb(es, "xs_t", [128, D], F32)
            if self.with_sample:
                S.memset("pool", self.xs_t, 0.0)
                S.dma("sp", self.xs_t[0:DEC, :], Tile(self.io["xs"], self.dbuf("xs")))
            n_gla = n_hgrn = n_diff = 0
            for li in range(4):
                kind = li % 3
                idx = (n_gla, n_hgrn, n_diff)[kind]
                if li in self.layers:
                    first = li == self.layers[0]
                    last = li == self.layers[-1]
                    if kind == 0:
                        self.layer_lin(li, "gla", idx, first, last)
                    elif kind == 1:
                        self.layer_lin(li, "hgrn", idx, first, last)
                    else:
                        self.layer_diff(li, idx, first, last)
                    S.barrier()
                if kind == 0:
                    n_gla += 1
                elif kind == 1:
                    n_hgrn += 1
                else:
                    n_diff += 1
            S.finish()
        return nc

    def dbuf(self, key):
        d = self.__dict__.setdefault("_dbufs", {})
        if key not in d:
            d[key] = Buf(key)
        return d[key]

    def _declare_io(self):
        T = self.T
        io = {}
        io["xp"] = self.din("xp", [T, D])
        io["xs"] = self.din("xs", [DEC, D])
        io["sg"] = self.din("sg", [2, 4, 256, 512])
        io["sh"] = self.din("sh", [1, 16, 128, 128])
        io["ck"] = self.din("ck", [PAST, D])
        io["cv"] = self.din("cv", [PAST, D])
        io["norm_w"] = self.din("norm_w", [4, D])
        io["final_norm_w"] = self.din("final_norm_w", [D])
        io["gla_w_in"] = self.din("gla_w_in", [2, D, 6144])
        io["gla_w_a1"] = self.din("gla_w_a1", [2, D, 16])
        io["gla_w_a2"] = self.din("gla_w_a2", [2, 16, 1024])
        io["gla_b_a"] = self.din("gla_b_a", [2, 1024])
        io["gla_norm_w"] = self.din("gla_norm_w", [2, 512])
        io["gla_w_out"] = self.din("gla_w_out", [2, D, D])
        io["hgrn_w_in"] = self.din("hgrn_w_in", [1, D, 8192])
        io["hgrn_lower_bounds"] = self.din("hgrn_lower_bounds", [4, D])
        io["hgrn_norm_w"] = self.din("hgrn_norm_w", [1, D])
        io["hgrn_w_out"] = self.din("hgrn_w_out", [1, D, D])
        io["diff_w_in"] = self.din("diff_w_in", [1, D, 8192])
        io["diff_lambda"] = self.din("diff_lambda", [1, 4, 128])
        io["diff_subln_w"] = self.din("diff_subln_w", [1, 256])
        io["diff_w_out"] = self.din("diff_w_out", [1, D, D])
        io["consts"] = self.din("consts", [128, 512])
        io["yp"] = self.dout("yp", [T, D])
        io["ys"] = self.dout("ys", [DEC, D])
        io["gsp"] = self.dout("gsp", [2, 4, 256, 512])
        io["gss"] = self.dout("gss", [2, 4, 256, 512])
        io["hsp"] = self.dout("hsp", [1, 16, 128, 128])
        io["hss"] = self.dout("hss", [1, 16, 128, 128])
        io["krp"] = self.dout("krp", [T, D])
        io["vrp"] = self.dout("vrp", [T, D])
        io["krs"] = self.dout("krs", [DEC, D])
        io["vrs"] = self.dout("vrs", [DEC, D])
        io["xres"] = self.dint("xres", [T, D], F32)
        io["wb_gla_in"] = self.dint("wb_gla_in", [2, 24, 128, KC * 256], BF16)
        io["wb_gla_out"] = self.dint("wb_gla_out", [2, 8, 128, KC * 256], BF16)
        io["wb_hgrn_in"] = self.dint("wb_hgrn_in", [1, 32, 128, KC * 256], BF16)
        io["wb_hgrn_out"] = self.dint("wb_hgrn_out", [1, 8, 128, KC * 256], BF16)
        io["wb_diff_in"] = self.dint("wb_diff_in", [1, 32, 128, KC * 256], BF16)
        io["wb_diff_out"] = self.dint("wb_diff_out", [1, 8, 128, KC * 256], BF16)
        self.io = io

    def _consts(self, es):
        S = self.S
        c = self.sb(es, "consts_f", [128, 512], F32)
        S.dma("sp", c, Tile(self.io["consts"], self.dbuf("consts")))
        self.ident_f = c[:, 0:128]
        self.mask_f = c[:, 128:256]
        self.ones_f = c[:, 256:384]
        self.ident = self.sb(es, "ident_b", [128, 128], BF16)
        S.copy("dve", self.ident, self.ident_f)
        self.eps_t = self.sb(es, "eps_t", [128, 1], F32)
        S.memset("pool", self.eps_t, EPS)

    def _prep_jobs(self, li):
        S = self.S
        io = self.io
        jobs = {0: [("gla_w_in", 0), ("gla_w_out", 0)], 1: [("hgrn_w_in", 0), ("hgrn_w_out", 0)],
                2: [("diff_w_in", 0), ("diff_w_out", 0)], 3: [("gla_w_in", 1), ("gla_w_out", 1)]}[li]
        out = []
        for name, j in jobs:
            src = io[name]
            dst = io["wb_" + name.replace("_w_", "_")]
            ncols = src.shape[2]
            for blk in range(ncols // 256):
                def job(src=src, dst=dst, name=name, j=j, blk=blk):
                    S.dma("pool", Tile(dst[j, blk].rearrange("p (kc c) -> p kc c", kc=KC), self.dbuf(("wb", name, j, blk))),
                          Tile(src[j, :, blk * 256:(blk + 1) * 256].rearrange("(kc p) c -> p kc c", p=128), self.dbuf(("w", name))))
                out.append(job)
        return out

    def prep_next(self, li):
        later = [l for l in self.layers if l > li]
        if later:
            self.prep_q = self.__dict__.get("prep_q", []) + self._prep_jobs(later[0])

    def prep_some(self, n):
        q = self.__dict__.get("prep_q", [])
        for _ in range(min(n, len(q))):
            q.pop(0)()

    def stage1_gen(self, L, xt_all, hT, TT, NS):
        S = self.S
        for j in range(NS):
            xt = xt_all[:, j, :]
            ss = L["ss_ring"].get()
            xn = L["xn_ring"].get()
            S.act(xn, xt, AF.Square, accum=ss[:, 0:1])
            S.act(ss[:, 1:2], ss[:, 0:1], AF.Ln, bias=self.eps_t, scale=1.0 / D)
            S.act(ss[:, 2:3], ss[:, 1:2], AF.Exp, scale=-0.5)
            S.act(xn, xt, AF.Copy, scale=ss[:, 2:3])
            yield
            for q in range(4):
                pt = L["ptr_ring"].get()
                for i in range(4):
                    kc = 4 * q + i
                    S.tr(pt[:, i, :], xn[:, kc * 128:(kc + 1) * 128], self.ident, last=(i == 3))
                S.tt("dve", hT[:, 4 * q:4 * q + 4, j * 128:(j + 1) * 128], pt[:, 0:4, :],
                     L["nw_fm"][:, 4 * q:4 * q + 4].unsq(2).bc([128, 4, 128]), ALU.mult)
                yield

    def stage1(self, L, xt_all, hT, TT, NS):
        for _ in self.stage1_gen(L, xt_all, hT, TT, NS):
            pass

    def wload(self, L, wsrc, key, c0, ncols=256):
        wt = L["wring"].get()
        blk = c0 // 256
        self.S.dma("sp", wt.re("p kc c -> p (kc c)"), Tile(wsrc[blk], self.dbuf(tuple(key) + (blk,))))
        return wt

    def fm_proj(self, L, wt, c, hT, TT, width=128):
        pp = L["pproj_ring"].get()
        for kc in range(KC):
            self.S.mm(pp[0:width, 0:TT], wt[:, kc, c * 128:c * 128 + width], hT[:, kc, 0:TT],
                      start=(kc == 0), stop=(kc == KC - 1))
        return pp

    def tm_proj(self, L, wt, hT, j, ncols=256):
        pp = L["pproj_ring"].get()
        for kc in range(KC):
            self.S.mm(pp[:, 0:ncols], hT[:, kc, j * 128:(j + 1) * 128], wt[:, kc, 0:ncols],
                      start=(kc == 0), stop=(kc == KC - 1))
        return pp

    def layer_lin(self, li, kind, idx, first, last):
        S = self.S
        io = self.io
        T, TT, NS = self.T, self.TT, self.NS
        gla = kind == "gla"
        KCg = 2 if gla else 4
        HG = 1 if gla else 4
        dkc = 2 if gla else 1
        dvh = 512 if gla else 128
        KCT = 8 if gla else 16
        qscale = (256.0 if gla else 128.0) ** -0.5
        gsc = (1.0 / 16.0) if gla else -1.0
        if gla:
            w_in = io["wb_gla_in"][idx]
            w_out = io["wb_gla_out"][idx]
            qoff, koff, voff, zoff = 0, 1024, 2048, 4096
            kin, kout = ("wb", "gla_w_in", idx), ("wb", "gla_w_out", idx)
        else:
            w_in = io["wb_hgrn_in"][idx]
            w_out = io["wb_hgrn_out"][idx]
            qoff, koff, voff, zoff = 0, 2048, 4096, 6144
            kin, kout = ("wb", "hgrn_w_in", idx), ("wb", "hgrn_w_out", idx)
        with ExitStack() as es:
            L = {}
            sb = lambda n, s, d: self.sb(es, "l%d_%s" % (li, n), s, d)
            psb = lambda n, s, d: self.ps(es, "l%d_%s" % (li, n), s, d)
            xt_all2 = [sb("xt%d" % i, [128, NS, D], F32) for i in range(2)]
            hT2 = [sb("hT%d" % i, [128, KC, TT], BF16) for i in range(2)]
            gT2 = [sb("gT%d" % i, [128, KC, TT], BF16) for i in range(2)]
            L["xn_ring"] = Ring([sb("xn%d" % i, [128, D], BF16) for i in range(1 if not gla else 2)])
            L["junk"] = None
            L["ss_ring"] = Ring([sb("ss%d" % i, [128, 4], F32) for i in range(4)])
            L["wring"] = Ring([sb("w%d" % i, [128, KC, 256], BF16) for i in range(3)])
            L["pproj_ring"] = Ring([psb("pp%d" % i, [128, 512], F32) for i in range(3)])
            L["ptr_ring"] = Ring([psb("pt%d" % i, [128, 8, 128], BF16) for i in range(2)])
            pch = Ring([psb("pc%d" % i, [128, 512], F32) for i in range(3)])
            nw_fm = self.load_fm(es, "l%d_nw_fm" % li, io["norm_w"][li].rearrange("(c p) -> c p", p=128), KC,
                                 "norm_w", L["pproj_ring"].get())
            L["nw_fm"] = nw_fm
            nw_b = sb("nw_b", [128, D], F32)
            if gla:
                for h in range(4):
                    S.dma("sp", nw_b[:, h * 512:(h + 1) * 512],
                          Tile(io["gla_norm_w"][idx].partition_broadcast(128), self.dbuf("gla_norm_w")))
            else:
                S.dma("sp", nw_b, Tile(io["hgrn_norm_w"][idx].partition_broadcast(128), self.dbuf("hgrn_norm_w")))
            fnw_b = None
            if last:
                fnw_b = sb("fnw_b", [128, D], F32)
                S.dma("sp", fnw_b, Tile(io["final_norm_w"].partition_broadcast(128), self.dbuf("final_norm_w")))
            Sst = sb("S", [128, KCT, dvh], F32)
            Sbf = sb("Sbf", [128, KCT, dvh], BF16)
            grp = []
            for i in range(2):
                grp.append(dict(
                    qT=sb("qT%d" % i, [128, KCg, TT], BF16),
                    kT=sb("kT%d" % i, [128, KCg, TT], BF16),
                    lg=sb("lg%d" % i, [128, KCg, TT], F32),
                    v=sb("v%d" % i, [128, NS, 512], BF16),
                    zs=sb("zs%d" % i, [128, NS, 512], BF16),
                ))
            tmp = []
            for i in range(2):
                tmp.append(dict(
                    Bl=sb("Bl%d" % i, [128, KCg, 128], F32),
                    Bm=sb("Bm%d" % i, [128, KCg, 128], F32),
                    Ei=sb("Ei%d" % i, [128, KCg, 128], F32),
                    Ea=sb("Ea%d" % i, [128, KCg, 128], F32),
                    Ek=sb("Ek%d" % i, [128, KCg, 128], F32),
                    Es=sb("Es%d" % i, [128, KCg, 128], F32),
                    qi=sb("qi%d" % i, [128, KCg, 128], BF16),
                    qa=sb("qa%d" % i, [128, KCg, 128], BF16),
                    ka=sb("ka%d" % i, [128, KCg, 128], BF16),
                    ksT=sb("ksT%d" % i, [128, KCg, 128], BF16),
                    ks=sb("ks%d" % i, [128, KCg, 128], BF16),
                    atT=sb("atT%d" % i, [128, HG, 128], BF16),
                    nwz=sb("nwz%d" % i, [128, 512], F32),
                    gt=sb("gt%d" % i, [128, 512], BF16),
                    dS=sb("dS%d" % i, [128, KCg, 1], F32),
                    osb=sb("osb%d" % i, [128, 512], F32),
                ))
            ssg2 = [sb("ssg%d" % i, [128, NS, 8], F32) for i in range(2)]
            lnq = sb("lnq", [128, 1], F32)
            S.memset("pool", lnq, math.log(qscale))
            rstd_tok2 = [sb("rstd_tok%d" % i, [128, NS, 2], F32) for i in range(2)]
            if gla:
                w_a1 = sb("w_a1", [128, KC, 16], BF16)
                S.dma("pool", w_a1, Tile(io["gla_w_a1"][idx].rearrange("(kc p) r -> p kc r", p=128),
                                         self.dbuf("gla_w_a1")))
                w_a2 = sb("w_a2", [32, 1024], F32)
                S.dma("sp", w_a2[0:16, :], Tile(io["gla_w_a2"][idx], self.dbuf("gla_w_a2")))
                S.dma("sp", w_a2[16:17, :], Tile(io["gla_b_a"][idx].unsqueeze(0), self.dbuf("gla_b_a")))
                raug2 = [sb("raug%d" % i, [32, TT], F32) for i in range(2)]
                for r_ in raug2:
                    S.memset("pool", r_, 1.0)
            else:
                lbp = self.load_fm(es, "l%d_lbp" % li, io["hgrn_lower_bounds"].rearrange("l (c p) -> (l c) p", p=128),
                                   4 * KC, "hgrn_lb", L["pproj_ring"].get()).re("p (l c) -> p l c", l=4)
                lbe = sb("lbe", [128, 4, KC], F32)
                S.act(lbe, lbp, AF.Exp)
                lbz = sb("lbz", [128, KC], F32)
                S.tt("dve", lbz, lbe[:, 0, :], lbe[:, 1, :], ALU.add)
                S.tt("dve", lbz, lbz, lbe[:, 2, :], ALU.add)
                S.tt("dve", lbz, lbz, lbe[:, 3, :], ALU.add)
                lbn = sb("lbn", [128, KC], F32)
                S.memset("pool", lbn, 0.0)
                for jj in range(1, li + 1):
                    S.tt("dve", lbn, lbn, lbe[:, jj, :], ALU.add)
                S.recip(lbz, lbz)
                S.tt("dve", lbn, lbn, lbz, ALU.mult)
                oml = sb("oml", [128, KC], F32)
                S.ts("dve", oml, lbn, -1.0, ALU.mult, 1.0, ALU.add)
                kf = [sb("kf%d" % i, [128, TT], F32) for i in range(2)]

            print("layer", li, kind, "sbuf bytes remaining", self.nc.sbuf_bytes_remaining, flush=True)
            def tile_A(c):
                x_src, TTc, NSc, valid = c["x_src"], c["TTc"], c["NSc"], c["valid"]
                xt_all, hT = xt_all2[c["p"]], hT2[c["p"]]
                if gla:
                    raug = raug2[c["p"]]
                if valid < 128:
                    S.copy("pool", xt_all[:, 0, :], self.xs_t)
                else:
                    for j in range(NSc):
                        S.dma("sp", xt_all[:, j, :], x_src(j))
                yield
                yield from self.stage1_gen(L, xt_all, hT, TTc, NSc)
                if gla:
                    pp = L["pproj_ring"].get()
                    for kc in range(KC):
                        S.mm(pp[0:16, 0:TTc], w_a1[:, kc, :], hT[:, kc, 0:TTc],
                             start=(kc == 0), stop=(kc == KC - 1))
                    S.copy("dve", raug[0:16, 0:TTc], pp[0:16, 0:TTc])
                yield

            def tile_B(c, g):
                    TTc, NSc, valid = c["TTc"], c["NSc"], c["valid"]
                    hT = hT2[c["p"]]
                    if gla:
                        raug = raug2[c["p"]]
                    G = grp[g % 2]
                    qT, kT, lg, v, zs = G["qT"], G["kT"], G["lg"], G["v"], G["zs"]
                    nblk = KCg // 2
                    for b in range(nblk):
                        wt = self.wload(L, w_in, kin, qoff + g * KCg * 128 + b * 256)
                        for c in range(2):
                            pp = self.fm_proj(L, wt, c, hT, TTc)
                            if gla:
                                self.evac(qT[:, 2 * b + c, 0:TTc], pp[:, 0:TTc])
                            else:
                                S.act(qT[:, 2 * b + c, 0:TTc], pp[:, 0:TTc], AF.Silu)
                            yield
                    for b in range(nblk):
                        wt = self.wload(L, w_in, kin, koff + g * KCg * 128 + b * 256)
                        for c in range(2):
                            kc = 2 * b + c
                            pp = self.fm_proj(L, wt, c, hT, TTc)
                            if gla:
                                self.evac(kT[:, kc, 0:TTc], pp[:, 0:TTc])
                            else:
                                kcg = g * KCg + kc
                                kff = kf[kc % 2]
                                S.act(kff[:, 0:TTc], pp[:, 0:TTc], AF.Sigmoid, scale=-1.0)
                                S.ts("dve", kff[:, 0:TTc], kff[:, 0:TTc], oml[:, kcg:kcg + 1], ALU.mult)
                                S.copy("pool", kT[:, kc, 0:TTc], kff[:, 0:TTc])
                                S.act(lg[:, kc, 0:TTc], kff[:, 0:TTc], AF.Ln, bias=1.0, scale=-1.0)
                            yield
                    if gla:
                        for kc in range(KCg):
                            kcg = g * KCg + kc
                            pp = L["pproj_ring"].get()
                            S.mm(pp[:, 0:TTc], w_a2[0:17, kcg * 128:(kcg + 1) * 128], raug[0:17, 0:TTc],
                                 start=True, stop=True)
                            S.act(lg[:, kc, 0:TTc], pp[:, 0:TTc], AF.Exp, scale=-1.0)
                            S.act(lg[:, kc, 0:TTc], lg[:, kc, 0:TTc], AF.Ln, bias=1.0)
                            yield
                    if valid < 128:
                        S.memset("pool", lg[:, :, valid:128], 0.0)
                    for b in range(2):
                        wt = self.wload(L, w_in, kin, voff + g * 512 + b * 256)
                        for j in range(NSc):
                            pp = self.tm_proj(L, wt, hT, j)
                            self.evac(v[:, j, b * 256:(b + 1) * 256], pp[:, 0:256])
                            yield
                    for b in range(2):
                        wt = self.wload(L, w_in, kin, zoff + g * 512 + b * 256)
                        for j in range(NSc):
                            pp = self.tm_proj(L, wt, hT, j)
                            S.act(zs[:, j, b * 256:(b + 1) * 256], pp[:, 0:256], AF.Silu)
                            yield

            def tile_C(c, g):
                    TTc, NSc, valid = c["TTc"], c["NSc"], c["valid"]
                    gT, ssg = gT2[c["p"]], ssg2[c["p"]]
                    G = grp[g % 2]
                    qT, kT, lg, v, zs = G["qT"], G["kT"], G["lg"], G["v"], G["zs"]
                    gens = [chunk(c, g, j) for j in range(NSc)]
                    if NSc == 1:
                        yield from gens[0]
                        return
                    ga, gb = gens
                    for k in range(3):
                        next(ga)
                        yield
                        next(gb)
                        yield
                    next(ga)
                    yield
                    next(ga)
                    yield
                    next(gb)
                    yield
                    next(gb)
                    yield
                    for _ in ga:
                        pass
                    for _ in gb:
                        pass

            def chunk(c, g, j):
                    TTc, NSc, valid = c["TTc"], c["NSc"], c["valid"]
                    gT, ssg = gT2[c["p"]], ssg2[c["p"]]
                    G = grp[g % 2]
                    qT, kT, lg, v, zs = G["qT"], G["kT"], G["lg"], G["v"], G["zs"]
                    if True:
                        W = tmp[j % 2]
                        cs = slice(j * 128, (j + 1) * 128)
                        Bl, Bm = W["Bl"], W["Bm"]
                        for kc in range(KCg):
                            S.scan(Bl[:, kc, :], self.ones_f, lg[:, kc, cs], 0.0, ALU.mult, ALU.add)
                        S.tt("dve", Bm, Bl, Bl[:, :, 63:64].bc([128, KCg, 128]), ALU.subtract)
                        S.act(W["Ea"], Bm, AF.Exp, scale=-gsc, bias=lnq)
                        S.act(W["Ek"], Bm, AF.Exp, scale=gsc)
                        S.act(W["Ei"], Bl, AF.Exp, scale=-gsc, bias=lnq)
                        S.act(W["dS"], Bl[:, :, 127:128], AF.Exp, scale=-gsc)
                        Bs = Bm
                        S.tt("pool", Bs, Bl, Bl[:, :, 127:128].bc([128, KCg, 128]), ALU.subtract)
                        S.act(W["Es"], Bs, AF.Exp, scale=gsc)
                        S.tt("dve", W["qi"], qT[:, :, cs], W["Ei"], ALU.mult)
                        S.tt("pool", W["qa"], qT[:, :, cs], W["Ea"], ALU.mult)
                        S.tt("dve", W["ka"], kT[:, :, cs], W["Ek"], ALU.mult)
                        S.tt("pool", W["ksT"], kT[:, :, cs], W["Es"], ALU.mult)
                        yield
                        pt = L["ptr_ring"].get()
                        for kc in range(KCg):
                            S.tr(pt[:, kc, :], W["ksT"][:, kc, :], self.ident, last=(kc == KCg - 1))
                        self.evac(W["ks"], pt[:, 0:KCg, :])
                        yield
                        pa = pch.get()
                        for h in range(HG if "noA" not in DBG else 1):
                            for c in range(dkc):
                                kc = h * dkc + c
                                S.mm(pa[:, h * 128:(h + 1) * 128], W["ka"][:, kc, :], W["qa"][:, kc, :],
                                     start=(c == 0), stop=(c == dkc - 1), last=(c == dkc - 1))
                        if "noM" in DBG:
                            for h in range(HG):
                                S.tt("dve", W["atT"][:, h, :], pa[:, h * 128:(h + 1) * 128], self.mask_f, ALU.mult)
                        else:
                            S.tt("dve", W["atT"], pa[:, 0:HG * 128].re("p (h t) -> p h t", h=HG),
                                 self.mask_f.unsq(1).bc([128, HG, 128]), ALU.mult)
                        yield
                        po = pch.get()
                        for h in range(HG if "noO" not in DBG else 1):
                            vs = slice(h * dvh, (h + 1) * dvh)
                            S.mm(po[:, vs], W["atT"][:, h, :], v[:, j, vs], start=True, stop=False, last=False)
                            for c in range(dkc):
                                kc = h * dkc + c
                                S.mm(po[:, vs], W["qi"][:, kc, :], Sbf[:, g * KCg + kc, :],
                                     start=False, stop=(c == dkc - 1), last=(c == dkc - 1))
                        gcols = slice(g * 512, (g + 1) * 512)
                        S.tt("pool", W["nwz"], zs[:, j, :], nw_b[:, gcols], ALU.mult)
                        osb = W["osb"]
                        S.copy("act", osb, po[:, 0:512])
                        S.act(W["gt"], osb, AF.Square, accum=ssg[:, j, g:g + 1])
                        if gla:
                            S.act(ssg[:, j, 4 + g:5 + g], ssg[:, j, g:g + 1], AF.Ln, bias=self.eps_t, scale=1.0 / 512)
                            S.act(ssg[:, j, 4 + g:5 + g], ssg[:, j, 4 + g:5 + g], AF.Exp, scale=-0.5)
                            S.stt("dve", W["gt"], osb, ssg[:, j, 4 + g:5 + g], W["nwz"], ALU.mult, ALU.mult)
                        else:
                            S.tt("dve", W["gt"], osb, W["nwz"], ALU.mult)
                        yield
                        pt = L["ptr_ring"].get()
                        for c in range(4):
                            S.tr(pt[:, c, :], W["gt"][:, c * 128:(c + 1) * 128], self.ident, last=(c == 3))
                        self.evac(gT[:, g * 4:(g + 1) * 4, cs], pt[:, 0:4, :])
                        for h in range(HG if "noS" not in DBG else 0):
                            vs = slice(h * dvh, (h + 1) * dvh)
                            for c in range(dkc):
                                kc = h * dkc + c
                                pu = pch.get()
                                S.mm(pu[:, 0:dvh], W["ks"][:, kc, :], v[:, j, vs], start=True, stop=True)
                                kcg = g * KCg + kc
                                S.stt("dve", Sst[:, kcg, :], Sst[:, kcg, :], W["dS"][:, kc, :],
                                      pu[:, 0:dvh], ALU.mult, ALU.add)
                                S.copy("pool", Sbf[:, kcg, :], Sst[:, kcg, :])
                        yield

            def tile_D(c):
                x_dst, y_dst, TTc, NSc, valid = c["x_dst"], c["y_dst"], c["TTc"], c["NSc"], c["valid"]
                xt_all, gT, ssg, rstd_tok = xt_all2[c["p"]], gT2[c["p"]], ssg2[c["p"]], rstd_tok2[c["p"]]
                if not gla:
                    for j in range(NSc):
                        S.tt("dve", ssg[:, j, 4:5], ssg[:, j, 0:1], ssg[:, j, 1:2], ALU.add)
                        S.tt("dve", ssg[:, j, 5:6], ssg[:, j, 2:3], ssg[:, j, 3:4], ALU.add)
                        S.tt("dve", ssg[:, j, 4:5], ssg[:, j, 4:5], ssg[:, j, 5:6], ALU.add)
                        S.act(rstd_tok[:, j, 0:1], ssg[:, j, 4:5], AF.Ln, bias=self.eps_t, scale=1.0 / D)
                        S.act(rstd_tok[:, j, 1:2], rstd_tok[:, j, 0:1], AF.Exp, scale=-0.5)
                for cb in range(8 if "noout" not in DBG else 0):
                    wt = self.wload(L, w_out, kout, cb * 256)
                    cols = slice(cb * 256, (cb + 1) * 256)
                    for j in range(NSc):
                        pp = L["pproj_ring"].get()
                        for ic in range(KC):
                            S.mm(pp[:, 0:256], gT[:, ic, j * 128:(j + 1) * 128], wt[:, ic, :],
                                 start=(ic == 0), stop=(ic == KC - 1))
                        if gla:
                            S.tt("dve", xt_all[:, j, cols], xt_all[:, j, cols], pp[:, 0:256], ALU.add)
                        else:
                            S.stt("dve", xt_all[:, j, cols], pp[:, 0:256], rstd_tok[:, j, 1:2],
                                  xt_all[:, j, cols], ALU.mult, ALU.add)
                        yield
                for j in range(NSc):
                    if valid < 128:
                        S.copy("pool", self.xs_t, xt_all[:, 0, :])
                    elif not last:
                        S.dma("sp", x_dst(j), xt_all[:, j, :])
                    if last:
                        self.final_norm(L, xt_all[:, j, :], fnw_b, y_dst(j), valid)


            def exhaust(gen):
                for _ in gen:
                    pass

            def merge(main, side, n_main, n_side):
                main, side = iter(main), iter(side)
                done_side = 0
                i = 0
                for _ in main:
                    i += 1
                    want = (i * n_side + n_main - 1) // n_main
                    while done_side < want:
                        done_side += 1
                        if next(side, "END") == "END":
                            done_side = 10 ** 9
                for _ in side:
                    pass

            def chain(*gens):
                for g_ in gens:
                    yield from g_

            def run_pipeline(ctxs, do_prep=False):
                n = len(ctxs)
                for i, c in enumerate(ctxs):
                    c["p"] = i % 2
                nB_ = lambda c: (6 if gla else 8) + 4 * c["NSc"]
                nD_ = lambda c: 8 * c["NSc"]
                nC_ = lambda c: 5 * c["NSc"]
                nA_ = lambda c: 2 + 5 * c["NSc"]
                exhaust(tile_A(ctxs[0]))
                side_prev, n_prev = iter(()), 0
                for t in range(n):
                    c = ctxs[t]
                    nB, nC = nB_(c), nC_(c)
                    merge(tile_B(c, 0), side_prev, nB, n_prev)
                    if "pre_C" in c:
                        c["pre_C"]()
                    mains = chain(tile_D(ctxs[t - 1]), tile_B(c, 1)) if t > 0 else tile_B(c, 1)
                    merge(mains, tile_C(c, 0), nB + (nD_(ctxs[t - 1]) if t > 0 else 0), nC)
                    genA = tile_A(ctxs[t + 1]) if t + 1 < n else iter(())
                    nA = nA_(ctxs[t + 1]) if t + 1 < n else 0
                    hA = nA // 2
                    merge(tile_B(c, 2), chain(tile_C(c, 1), itertools.islice(genA, hA)), nB, nC + hA)
                    merge(tile_B(c, 3), chain(tile_C(c, 2), genA), nB, nC + nA - hA)
                    side_prev, n_prev = tile_C(c, 3), nC
                    if t == 0 and do_prep:
                        self.prep_next(li)
                    self.prep_some(4)
                exhaust(side_prev)
                self.prep_some(1000)
                exhaust(tile_D(ctxs[n - 1]))

            S.memset("pool", Sst, 0.0)
            S.memset("pool", Sbf, 0.0)
            xsrc_ap = io["xp"] if first else io["xres"]
            xsrc_key = "xp" if first else "xres"
            ctxs = []
            for ti in range(T // TT):
                r0 = ti * TT
                ctxs.append(dict(
                    x_src=lambda j, r0=r0: Tile(xsrc_ap[r0 + j * 128:r0 + (j + 1) * 128, :], self.dbuf((xsrc_key, r0 + j * 128))),
                    x_dst=lambda j, r0=r0: Tile(io["xres"][r0 + j * 128:r0 + (j + 1) * 128, :], self.dbuf(("xres", r0 + j * 128))),
                    y_dst=lambda j, r0=r0: Tile(io["yp"][r0 + j * 128:r0 + (j + 1) * 128, :], self.dbuf(("yp", r0 + j * 128))),
                    TTc=TT, NSc=NS, valid=128))
            def swap_state():
                if gla:
                    S.dma("sp", Tile(io["gsp"][idx].rearrange("h (c p) v -> p (h c) v", p=128), self.dbuf(("gsp", idx))), Sst)
                elif "nostate" not in DBG:
                    S.dma("sp", Tile(io["hsp"][idx].rearrange("h p v -> p h v"), self.dbuf(("hsp", idx))), Sst)
                if self.with_sample:
                    if gla:
                        S.dma("sp", Sst, Tile(io["sg"][idx].rearrange("h (c p) v -> p (h c) v", p=128), self.dbuf("sg")))
                    else:
                        S.dma("sp", Sst, Tile(io["sh"][idx].rearrange("h p v -> p h v"), self.dbuf("sh")))
                    S.copy("pool", Sbf, Sst)

            if self.with_sample:
                ctxs.append(dict(x_src=None, x_dst=None, y_dst=lambda j: Tile(io["ys"], self.dbuf("ys")),
                                 TTc=128, NSc=1, valid=DEC, pre_C=swap_state))
                run_pipeline(ctxs, do_prep=True)
                if gla:
                    S.dma("sp", Tile(io["gss"][idx].rearrange("h (c p) v -> p (h c) v", p=128), self.dbuf(("gss", idx))), Sst)
                else:
                    S.dma("sp", Tile(io["hss"][idx].rearrange("h p v -> p h v"), self.dbuf(("hss", idx))), Sst)
            else:
                run_pipeline(ctxs, do_prep=True)
                swap_state()
            S.barrier()

    def final_norm(self, L, xt, fnw_b, ydst, valid):
        S = self.S
        ss = L["ss_ring"].get()
        junk = L["junk"] if L.get("junk") is not None else L["xn_ring"].get()
        S.act(junk, xt, AF.Square, accum=ss[:, 0:1])
        S.act(ss[:, 1:2], ss[:, 0:1], AF.Ln, bias=self.eps_t, scale=1.0 / D)
        S.act(ss[:, 2:3], ss[:, 1:2], AF.Exp, scale=-0.5)
        S.stt("dve", xt, xt, ss[:, 2:3], fnw_b, ALU.mult, ALU.mult)
        if valid < 128:
            S.dma("sp", ydst, xt[0:valid, :])
        else:
            S.dma("sp", ydst, xt)

    def layer_diff(self, li, idx, first, last):
        S = self.S
        io = self.io
        T, TT, NS = self.T, self.TT, self.NS
        TP = T + 128
        NKT = T // 128
        lam_init = 0.8 - 0.6 * math.exp(-0.3 * li)
        w_in = io["wb_diff_in"][idx]
        w_out = io["wb_diff_out"][idx]
        kin, kout = ("wb", "diff_w_in", idx), ("wb", "diff_w_out", idx)
        qT_d = self.dint("qT_d", [16, 128, TP], BF16)
        kT_d = self.dint("kT_d", [16, 128, TP], BF16)
        v_d = self.dint("v_d", [TP, D], BF16)
        z_d = self.dint("z_d", [TP, D], BF16)
        o_d = self.dint("o_d", [TP, D], BF16)
        ntile = T // TT
        tb = lambda name, r0: self.dbuf((name, r0 // TT if r0 < T else ntile))
        all_tb = lambda name: [self.dbuf((name, i)) for i in range(ntile + 1)]
        with ExitStack() as es:
            sb = lambda n, s, d: self.sb(es, "l%d_%s" % (li, n), s, d)
            stat = sb("stat", [128, 64], F32)
            qmax = sb("qmax", [128, 16], F32)
            kmax = sb("kmax", [128, 16], F32)
            S.memset("pool", qmax, 0.0)
            S.memset("pool", kmax, 0.0)
            negc = sb("negc", [128, 1], F32)
            neglam = sb("neglam", [128, 1], F32)
            slw_b = sb("slw_b", [128, 256], F32)
            S.dma("sp", slw_b, Tile(io["diff_subln_w"][idx].partition_broadcast(128), self.dbuf("diff_subln_w")))
            S.act(slw_b, slw_b, AF.Copy, scale=1.0 - lam_init)
            lamb = sb("lamb", [128, 4, 128], F32)
            S.dma("sp", lamb, Tile(io["diff_lambda"][idx].rearrange("a b -> (a b)").partition_broadcast(128),
                                   self.dbuf("diff_lambda")))
            lp = sb("lp", [128, 2, 128], F32)
            S.tt("dve", lp[:, 0, :], lamb[:, 0, :], lamb[:, 1, :], ALU.mult)
            S.tt("dve", lp[:, 1, :], lamb[:, 2, :], lamb[:, 3, :], ALU.mult)
            S.op("dve", lambda e: e.reduce_sum(out=stat.ap[:, 0:2], in_=lp.ap, axis=AX.X), [lp], [stat])
            S.act(stat[:, 2:4], stat[:, 0:2], AF.Exp)
            S.tt("dve", stat[:, 4:5], stat[:, 3:4], stat[:, 2:3], ALU.subtract)
            S.ts("dve", neglam, stat[:, 4:5], -lam_init, ALU.add)
            kTc = sb("kTc", [128, 16, PAST + 128], BF16)
            vc = sb("vc", [128, PAST // 128 + 1, 8, 257], BF16)
            S.memset("pool", vc[:, :, :, 256:257], 1.0)

            with ExitStack() as ea:
                L = {}
                sa = lambda n, s, d: self.sb(ea, "l%dA_%s" % (li, n), s, d)
                psa = lambda n, s, d: self.ps(ea, "l%dA_%s" % (li, n), s, d)
                xt_all = sa("xt", [128, NS, D], F32)
                hT = sa("hT", [128, KC, TT], BF16)
                L["junk"] = sa("junk", [128, D], BF16)
                L["xn_ring"] = Ring([sa("xn%d" % i, [128, D], BF16) for i in range(2)])
                L["ss_ring"] = Ring([sa("ss%d" % i, [128, 4], F32) for i in range(4)])
                L["wring"] = Ring([sa("w%d" % i, [128, KC, 256], BF16) for i in range(3)])
                L["pproj_ring"] = Ring([psa("pp%d" % i, [128, 512], F32) for i in range(3)])
                L["ptr_ring"] = Ring([psa("pt%d" % i, [128, 8, 128], BF16) for i in range(2)])
                L["nw_fm"] = self.load_fm(ea, "l%d_nw_fm" % li, io["norm_w"][li].rearrange("(c p) -> c p", p=128), KC,
                                          "norm_w", L["pproj_ring"].get())
                qT_t = sa("qT_t", [128, 16, TT], BF16)
                kT_t = sa("kT_t", [128, 16, TT], BF16)
                v_t = sa("v_t", [128, NS, D], BF16)
                z_t = sa("z_t", [128, NS, D], BF16)
                f32r = Ring([sa("f32r%d" % i, [128, 256], F32) for i in range(3)])
                b16r = Ring([sa("b16r%d" % i, [128, 256], BF16) for i in range(3)])
                sqr = Ring([sa("sqr%d" % i, [128, 256], F32) for i in range(2)])
                nrm = Ring([sa("nrm%d" % i, [128, 2], F32) for i in range(4)])
                ckf = sa("ckf", [128, D], F32)
                ckb = sa("ckb", [128, D], BF16)
                cksq = L["junk"]

                def qk_block(pp, b, j, TTc, dstT, mx, scale, out_rows, valid):
                    if out_rows is not None:
                        f = f32r.get()
                        S.copy("act", f, pp[:, 0:256])
                        if valid < 128:
                            S.dma("act", Tile(out_rows[0].ap[0:valid, b * 256:(b + 1) * 256], out_rows[0].buf), f[0:valid, :])
                        else:
                            S.dma("act", Tile(out_rows[0].ap[:, b * 256:(b + 1) * 256], out_rows[0].buf), f)
                    bb = b16r.get()
                    if out_rows is not None:
                        S.copy("dve", bb, f)
                    elif scale != 1.0:
                        S.ts("dve", bb, pp[:, 0:256], scale, ALU.mult)
                    else:
                        S.copy("dve", bb, pp[:, 0:256])
                    sq = sqr.get()
                    S.tt("pool", sq, bb, bb, ALU.mult)
                    nn = nrm.get()
                    S.op("dve", lambda e: e.reduce_sum(out=nn.ap, in_=sq.ap.rearrange("p (n d) -> p n d", n=2), axis=AX.X),
                         [sq], [nn])
                    S.tt("dve", mx[:, 2 * b:2 * b + 2], mx[:, 2 * b:2 * b + 2], nn, ALU.max)

                    def later():
                        pt = L["ptr_ring"].get()
                        for n in range(2):
                            S.tr(pt[:, n, :], bb[:, n * 128:(n + 1) * 128], self.ident, last=(n == 1))
                        self.evac(dstT[:, 2 * b:2 * b + 2, j * 128:(j + 1) * 128], pt[:, 0:2, :])
                    pendA.append(later)

                pendA = []

                def flushA(keep=0):
                    while len(pendA) > keep:
                        pendA.pop(0)()

                def phaseA_tile(x_src, r0, TTc, NSc, valid, krows, vrows):
                    if valid < 128:
                        S.copy("pool", xt_all[:, 0, :], self.xs_t)
                    else:
                        for j in range(NSc):
                            S.dma("sp", xt_all[:, j, :], x_src(j))
                    self.stage1(L, xt_all, hT, TTc, NSc)
                    for b in range(8):
                        wt = self.wload(L, w_in, kin, b * 256)
                        for j in range(NSc):
                            pp = self.tm_proj(L, wt, hT, j)
                            flushA()
                            qk_block(pp, b, j, TTc, qT_t, qmax, 128.0 ** -0.5, None, valid)
                    for b in range(8):
                        wt = self.wload(L, w_in, kin, D + b * 256)
                        for j in range(NSc):
                            pp = self.tm_proj(L, wt, hT, j)
                            flushA()
                            qk_block(pp, b, j, TTc, kT_t, kmax, 1.0, (krows(j),), valid)
                    for b in range(8):
                        wt = self.wload(L, w_in, kin, 2 * D + b * 256)
                        for j in range(NSc):
                            pp = self.tm_proj(L, wt, hT, j)
                            flushA()
                            f = f32r.get()
                            S.copy("act", f, pp[:, 0:256])
                            vr = vrows(j)
                            if valid < 128:
                                S.dma("act", Tile(vr.ap[0:valid, b * 256:(b + 1) * 256], vr.buf), f[0:valid, :])
                            else:
                                S.dma("act", Tile(vr.ap[:, b * 256:(b + 1) * 256], vr.buf), f)
                            S.copy("dve", v_t[:, j, b * 256:(b + 1) * 256], f)
                    for b in range(8):
                        wt = self.wload(L, w_in, kin, 3 * D + b * 256)
                        for j in range(NSc):
                            pp = self.tm_proj(L, wt, hT, j)
                            S.act(z_t[:, j, b * 256:(b + 1) * 256], pp[:, 0:256], AF.Silu)
                    S.dma("act", Tile(qT_d[:, :, r0:r0 + TTc].rearrange("n d t -> d n t"), tb("qT_d", r0)), qT_t[:, :, 0:TTc])
                    S.dma("act", Tile(kT_d[:, :, r0:r0 + TTc].rearrange("n d t -> d n t"), tb("kT_d", r0)), kT_t[:, :, 0:TTc])
                    S.dma("act", Tile(v_d[r0:r0 + TTc, :].rearrange("(j p) c -> p j c", p=128), tb("v_d", r0)), v_t[:, 0:NSc, :])
                    S.dma("act", Tile(z_d[r0:r0 + TTc, :].rearrange("(j p) c -> p j c", p=128), tb("z_d", r0)), z_t[:, 0:NSc, :])

                xsrc_ap = io["xp"] if first else io["xres"]
                xsrc_key = "xp" if first else "xres"
                for ti in range(ntile):
                    r0 = ti * TT
                    phaseA_tile(lambda j: Tile(xsrc_ap[r0 + j * 128:r0 + (j + 1) * 128, :], self.dbuf((xsrc_key, r0 + j * 128))),
                                r0, TT, NS, 128,
                                lambda j: Tile(io["krp"][r0 + j * 128:r0 + (j + 1) * 128, :], self.dbuf(("krp", r0 + j * 128))),
                                lambda j: Tile(io["vrp"][r0 + j * 128:r0 + (j + 1) * 128, :], self.dbuf(("vrp", r0 + j * 128))))
                    if ti == 0:
                        self.prep_next(li)
                    self.prep_some(4)
                self.prep_some(1000)
                if self.with_sample:
                    phaseA_tile(None, T, 128, 1, DEC,
                                lambda j: Tile(io["krs"], self.dbuf("krs")),
                                lambda j: Tile(io["vrs"], self.dbuf("vrs")))
                    for kt in range(PAST // 128):
                        S.dma("sp", ckf, Tile(io["ck"][kt * 128:(kt + 1) * 128, :], self.dbuf("ck")))
                        S.copy("act", ckb, ckf)
                        S.tt("pool", ckf, ckf, ckf, ALU.mult)
                        nn = sa("cnn%d" % kt, [128, 16], F32)
                        S.op("dve", lambda e, nn=nn: e.reduce_sum(out=nn.ap, in_=ckf.ap.rearrange("p (n d) -> p n d", n=16), axis=AX.X),
                             [ckf], [nn])
                        S.tt("dve", kmax, kmax, nn, ALU.max)
                        for q4 in range(4):
                            pt = L["ptr_ring"].get()
                            for i in range(4):
                                n = 4 * q4 + i
                                S.tr(pt[:, i, :], ckb[:, n * 128:(n + 1) * 128], self.ident, last=(i == 3))
                            self.evac(kTc[:, 4 * q4:4 * q4 + 4, kt * 128:(kt + 1) * 128], pt[:, 0:4, :])
                        S.dma("sp", ckf, Tile(io["cv"][kt * 128:(kt + 1) * 128, :], self.dbuf("cv")))
                        S.copy("dve", vc[:, kt, :, 0:256], ckf.re("p (h c) -> p h c", h=8))
                pp = L["pproj_ring"].get()
                S.rmax(stat[:, 8:9], qmax)
                S.rmax(stat[:, 9:10], kmax)
                S.tr(pp[0:2, 0:128], stat[:, 8:10], self.ident_f)
                S.copy("dve", stat[0:2, 16:48].bc([2, 32]) if False else stat[0:2, 10:11], stat[0:2, 10:11]) if False else None
                mx2 = sa("mx2", [2, 128], F32)
                S.copy("dve", mx2, pp[0:2, 0:128])
                S.rmax(stat[0:2, 10:11], mx2)
                S.act(stat[0:2, 11:12], stat[0:2, 10:11], AF.Ln)
                pp2 = L["pproj_ring"].get()
                S.mm(pp2[:, 0:1], self.ones_f[0:2, 0:128], stat[0:2, 11:12], start=True, stop=True)
                S.act(stat[:, 12:13], pp2[:, 0:1], AF.Exp, scale=0.5)
                S.ts("dve", negc, stat[:, 12:13], -1.0, ALU.mult)
            S.barrier()

            if DBG:
                print("phase A end n_inst", S.n_inst, flush=True)
            def attend(ebs, qT_h, kT_h, v_of_kt, nkt_of_qt, n_qt, mask_kind, ps_ring, po_ring, p_ring, fin, o_dst):
                pend = []

                def flush(keep):
                    while len(pend) > keep:
                        pend.pop(0)()

                for qt in range(n_qt):
                    po1 = po_ring.get()
                    po2 = po_ring.get()
                    nkt = nkt_of_qt(qt)
                    for kt in range(nkt):
                        ps = ps_ring.get()
                        for n in range(2):
                            S.mm(ps[:, n * 128:(n + 1) * 128], kT_h[:, n, kt * 128:(kt + 1) * 128],
                                 qT_h[:, n, qt * 128:(qt + 1) * 128], start=True, stop=True)
                        p = p_ring.get()
                        S.act(p, ps[:, 0:256], AF.Exp, bias=negc)
                        if kt == nkt - 1:
                            if mask_kind == "causal":
                                S.memset("pool", p.re("p (n t) -> p n t", n=2)[64:128, :, 0:64], 0.0)
                            else:
                                S.memset("pool", p[32:64, :], 0.0)
                                S.memset("pool", p[64:128, :], 0.0)
                        flush(2)

                        def pv(p=p, kt=kt, nkt=nkt, po1=po1, po2=po2):
                            S.mm(po1[:, 0:257], p[:, 0:128], v_of_kt(kt), start=(kt == 0), stop=(kt == nkt - 1))
                            S.mm(po2[:, 0:257], p[:, 128:256], v_of_kt(kt), start=(kt == 0), stop=(kt == nkt - 1))
                        pend.append(pv)

                    def finalize(qt=qt, po1=po1, po2=po2):
                        F = fin.get()
                        S.recip(F["r"][:, 0:1], po1[:, 256:257])
                        S.ts("dve", F["on"], po1[:, 0:256], F["r"][:, 0:1], ALU.mult)
                        S.recip(F["r"][:, 1:2], po2[:, 256:257])
                        S.tt("dve", F["r"][:, 1:2], F["r"][:, 1:2], neglam, ALU.mult)
                        S.stt("dve", F["on"], po2[:, 0:256], F["r"][:, 1:2], F["on"], ALU.mult, ALU.add)
                        S.act(F["sq"], F["on"], AF.Square, accum=F["r"][:, 2:3])
                        S.act(F["r"][:, 3:4], F["r"][:, 2:3], AF.Ln, bias=self.eps_t, scale=1.0 / 256)
                        S.act(F["r"][:, 4:5], F["r"][:, 3:4], AF.Exp, scale=-0.5)
                        S.stt("dve", F["ob"], F["on"], F["r"][:, 4:5], slw_b, ALU.mult, ALU.mult)
                        S.dma("pool", o_dst(qt), F["ob"])
                    pend.append(finalize)
                flush(0)

            with ExitStack() as eb:
                sbb = lambda n, s, d: self.sb(eb, "l%dB_%s" % (li, n), s, d)
                psb_ = lambda n, s, d: self.ps(eb, "l%dB_%s" % (li, n), s, d)
                ps_ring = Ring([psb_("ps%d" % i, [128, 512], F32) for i in range(4)])
                po_ring = Ring([psb_("po%d" % i, [128, 512], F32) for i in range(4)])
                p_ring = Ring([sbb("p%d" % i, [128, 256], BF16) for i in range(6)])
                fin = Ring([dict(r=sbb("fr%d" % i, [128, 8], F32), on=sbb("fon%d" % i, [128, 256], F32),
                                 sq=sbb("fsq%d" % i, [128, 256], BF16), ob=sbb("fob%d" % i, [128, 256], BF16))
                            for i in range(2)])
                hb = []
                for i in range(2):
                    hb.append(dict(q=sbb("qTh%d" % i, [128, 2, T], BF16), k=sbb("kTh%d" % i, [128, 2, T], BF16),
                                   v=sbb("vh%d" % i, [128, NKT, 257], BF16)))
                    S.memset("pool", hb[i]["v"][:, :, 256:257], 1.0)
                for h in range(8):
                    H = hb[h % 2]
                    S.dma("sp", H["q"], Tile(qT_d[2 * h:2 * h + 2, :, 0:T].rearrange("n d t -> d n t"), self.dbuf(("qT_d", 0))),
                          extra_reads=all_tb("qT_d"))
                    S.dma("sp", H["k"], Tile(kT_d[2 * h:2 * h + 2, :, 0:T].rearrange("n d t -> d n t"), self.dbuf(("kT_d", 0))),
                          extra_reads=all_tb("kT_d"))
                    for k0 in range(0, NKT, 8):
                        k1 = min(NKT, k0 + 8)
                        S.dma("sp", H["v"][:, k0:k1, 0:256],
                              Tile(v_d[k0 * 128:k1 * 128, h * 256:(h + 1) * 256].rearrange("(kt p) c -> p kt c", p=128),
                                   self.dbuf(("v_d", 0))), extra_reads=all_tb("v_d"))
                    attend(eb, H["q"], H["k"], lambda kt, H=H: H["v"][:, kt, :], lambda qt: qt + 1, NKT, "causal",
                           ps_ring, po_ring, p_ring, fin,
                           lambda qt, h=h: Tile(o_d[qt * 128:(qt + 1) * 128, h * 256:(h + 1) * 256],
                                                self.dbuf(("o_d", (qt * 128) // TT))))
                if self.with_sample:
                    qTs = sbb("qTs", [128, 16, 128], BF16)
                    S.dma("sp", qTs, Tile(qT_d[:, :, T:TP].rearrange("n d t -> d n t"), self.dbuf(("qT_d", ntile))))
                    S.dma("sp", kTc[:, :, PAST:PAST + 128], Tile(kT_d[:, :, T:TP].rearrange("n d t -> d n t"), self.dbuf(("kT_d", ntile))))
                    S.dma("sp", vc[:, PAST // 128, :, 0:256], Tile(v_d[T:TP, :].rearrange("p (h c) -> p h c", h=8), self.dbuf(("v_d", ntile))))
                    nk = PAST // 128 + 1
                    for h in range(8):
                        attend(eb, qTs[:, 2 * h:2 * h + 2, :], kTc[:, 2 * h:2 * h + 2, :], lambda kt, h=h: vc[:, kt, h, :],
                               lambda qt: nk, 1, "pad", ps_ring, po_ring, p_ring, fin,
                               lambda qt, h=h: Tile(o_d[T:TP, h * 256:(h + 1) * 256], self.dbuf(("o_d", ntile))))
            S.barrier()

            if DBG:
                print("phase B end n_inst", S.n_inst, flush=True)
            with ExitStack() as ec:
                L = {}
                sc = lambda n, s, d: self.sb(ec, "l%dC_%s" % (li, n), s, d)
                psc = lambda n, s, d: self.ps(ec, "l%dC_%s" % (li, n), s, d)
                xt_all = sc("xt", [128, NS, D], F32)
                gT = sc("gT", [128, KC, TT], BF16)
                o_t = sc("o_t", [128, NS, D], BF16)
                z_t = sc("z_t", [128, NS, D], BF16)
                L["junk"] = sc("junk", [128, D], BF16)
                L["ss_ring"] = Ring([sc("ss%d" % i, [128, 4], F32) for i in range(4)])
                L["wring"] = Ring([sc("w%d" % i, [128, KC, 256], BF16) for i in range(3)])
                L["pproj_ring"] = Ring([psc("pp%d" % i, [128, 512], F32) for i in range(3)])
                L["ptr_ring"] = Ring([psc("pt%d" % i, [128, 8, 128], BF16) for i in range(2)])
                fnw_b = None
                if last:
                    fnw_b = sc("fnw_b", [128, D], F32)
                    S.dma("sp", fnw_b, Tile(io["final_norm_w"].partition_broadcast(128), self.dbuf("final_norm_w")))

                def phaseC_tile(x_src, x_dst, y_dst, r0, TTc, NSc, valid):
                    if valid < 128:
                        S.copy("pool", xt_all[:, 0, :], self.xs_t)
                    else:
                        for j in range(NSc):
                            S.dma("sp", xt_all[:, j, :], x_src(j))
                    S.dma("sp", o_t[:, 0:NSc, :], Tile(o_d[r0:r0 + TTc, :].rearrange("(j p) c -> p j c", p=128), tb("o_d", r0)))
                    S.dma("sp", z_t[:, 0:NSc, :], Tile(z_d[r0:r0 + TTc, :].rearrange("(j p) c -> p j c", p=128), tb("z_d", r0)))
                    for j in range(NSc):
                        S.tt("pool", o_t[:, j, :], o_t[:, j, :], z_t[:, j, :], ALU.mult)
                        for q4 in range(4):
                            pt = L["ptr_ring"].get()
                            for i in range(4):
                                c = 4 * q4 + i
                                S.tr(pt[:, i, :], o_t[:, j, c * 128:(c + 1) * 128], self.ident, last=(i == 3))
                            self.evac(gT[:, 4 * q4:4 * q4 + 4, j * 128:(j + 1) * 128], pt[:, 0:4, :])
                    for cb in range(8):
                        wt = self.wload(L, w_out, kout, cb * 256)
                        cols = slice(cb * 256, (cb + 1) * 256)
                        for j in range(NSc):
                            pp = L["pproj_ring"].get()
                            for ic in range(KC):
                                S.mm(pp[:, 0:256], gT[:, ic, j * 128:(j + 1) * 128], wt[:, ic, :],
                                     start=(ic == 0), stop=(ic == KC - 1))
                            S.tt("dve", xt_all[:, j, cols], xt_all[:, j, cols], pp[:, 0:256], ALU.add)
                    for j in range(NSc):
                        if valid < 128:
                            S.copy("pool", self.xs_t, xt_all[:, 0, :])
                        elif not last:
                            S.dma("sp", x_dst(j), xt_all[:, j, :])
                        if last:
                            self.final_norm(L, xt_all[:, j, :], fnw_b, y_dst(j), valid)

                xsrc_ap = io["xp"] if first else io["xres"]
                xsrc_key = "xp" if first else "xres"
                for ti in range(ntile):
                    r0 = ti * TT
                    phaseC_tile(lambda j: Tile(xsrc_ap[r0 + j * 128:r0 + (j + 1) * 128, :], self.dbuf((xsrc_key, r0 + j * 128))),
                                lambda j: Tile(io["xres"][r0 + j * 128:r0 + (j + 1) * 128, :], self.dbuf(("xres", r0 + j * 128))),
                                lambda j: Tile(io["yp"][r0 + j * 128:r0 + (j + 1) * 128, :], self.dbuf(("yp", r0 + j * 128))),
                                r0, TT, NS, 128)
                if self.with_sample:
                    phaseC_tile(None, None, lambda j: Tile(io["ys"], self.dbuf("ys")), T, 128, 1, DEC)
            S.barrier()


def make_consts():
    c = np.zeros((128, 512), np.float32)
    c[:, 0:128] = np.eye(128, dtype=np.float32)
    s = np.arange(128)
    c[:, 128:256] = (s[:, None] <= s[None, :]).astype(np.float32)
    c[:, 256:384] = 1.0
    return c


def make_in_maps(inputs, n_cores=8, T=4096):
    consts = make_consts()
    maps = []
    f = lambda a: np.ascontiguousarray(np.asarray(a, dtype=np.float32))
    shared = {k: f(inputs[k]) for k in
              ["norm_w", "final_norm_w", "gla_w_in", "gla_w_a1", "gla_w_a2", "gla_b_a", "gla_norm_w",
               "gla_w_out", "hgrn_w_in", "hgrn_lower_bounds", "hgrn_norm_w", "hgrn_w_out",
               "diff_w_in", "diff_lambda", "diff_subln_w", "diff_w_out"]}
    xp = f(inputs["x_prompt"])
    xs = f(inputs["x_sample"])
    sg = f(inputs["state_gla"])
    sh = f(inputs["state_hgrn"])
    ck = f(inputs["cache_k"])
    cv = f(inputs["cache_v"])
    nb = xp.shape[0]
    for c in range(n_cores):
        m = dict(shared)
        m["xp"] = np.ascontiguousarray(xp[c % nb, :T])
        m["xs"] = np.ascontiguousarray(xs[c])
        m["sg"] = np.ascontiguousarray(sg[:, c])
        m["sh"] = np.ascontiguousarray(sh[:, c])
        m["ck"] = np.ascontiguousarray(ck[0, c].reshape(PAST, D))
        m["cv"] = np.ascontiguousarray(cv[0, c].reshape(PAST, D))
        m["consts"] = consts
        maps.append(m)
    return maps


_PROG_CACHE = {}


def kernel(**inputs):
    T = 4096
    if "full" not in _PROG_CACHE:
        _PROG_CACHE["full"] = Prog(T=T).build()
    nc = _PROG_CACHE["full"]
    maps = make_in_maps(inputs, 8, T)
    res = run_bass_kernel_spmd(nc, maps, core_ids=list(range(8)))
    R = res.results
    B = 4
    y_prompt = np.stack([R[b]["yp"] for b in range(B)])
    y_sample = np.stack([R[c]["ys"] for c in range(8)])
    gsp = np.stack([R[b]["gsp"] for b in range(B)], axis=1)
    gss = np.stack([R[c]["gss"] for c in range(8)], axis=1)
    hsp = np.stack([R[b]["hsp"] for b in range(B)], axis=1)
    hss = np.stack([R[c]["hss"] for c in range(8)], axis=1)
    krp = np.stack([R[b]["krp"] for b in range(B)])[None].reshape(1, B, T, 16, 128)
    vrp = np.stack([R[b]["vrp"] for b in range(B)])[None].reshape(1, B, T, 8, 256)
    krs = np.stack([R[c]["krs"] for c in range(8)])[None].reshape(1, 8, DEC, 16, 128)
    vrs = np.stack([R[c]["vrs"] for c in range(8)])[None].reshape(1, 8, DEC, 8, 256)
    return (y_prompt, y_sample, gsp, gss, hsp, hss, krp, vrp, krs, vrs)
```

```python
import itertools
import math
from contextlib import ExitStack

import numpy as np
import concourse.bass as bass
import concourse.mybir as mybir
from concourse.bass_utils import run_bass_kernel_spmd

F32 = mybir.dt.float32
BF16 = mybir.dt.bfloat16
AF = mybir.ActivationFunctionType
ALU = mybir.AluOpType
AX = mybir.AxisListType

DBG = ""
LIMIT = None
TRACE_RANGE = None
D = 2048
KC = 16
EPS = 1e-6
PAST = 1024
DEC = 32


class Buf:
    __slots__ = ("name", "w", "r", "excl")

    def __init__(self, name="", excl=False):
        self.name = name
        self.w = None
        self.r = {}
        self.excl = excl


class Tile:
    __slots__ = ("ap", "buf")

    def __init__(self, ap, buf=None, name=""):
        self.ap = ap
        self.buf = buf if buf is not None else Buf(name)

    def __getitem__(self, key):
        return Tile(self.ap[key], self.buf)

    def re(self, pat, **kw):
        return Tile(self.ap.rearrange(pat, **kw), self.buf)

    def bc(self, shape):
        return Tile(self.ap.to_broadcast(list(shape)), self.buf)

    def unsq(self, axis):
        return Tile(self.ap.unsqueeze(axis), self.buf)

    def sub(self, key, name=""):
        return Tile(self.ap[key], Buf(name))


def _ap(x):
    return x.ap if isinstance(x, Tile) else x


class Sched:
    def __init__(self, nc, es, n_dma=32, same_sync=True):
        self.nc = nc
        self.eng = {"pe": nc.tensor, "act": nc.scalar, "dve": nc.vector,
                    "pool": nc.gpsimd, "sp": nc.sync}
        self.sem = {k: es.enter_context(nc.semaphore("sem_" + k)) for k in self.eng}
        self.cnt = {k: 0 for k in self.eng}
        self.seen = {k: {} for k in self.eng}
        self.dsem = [es.enter_context(nc.semaphore("dsem%d" % i)) for i in range(n_dma)]
        self.dcnt = [0] * n_dma
        self.n_sw = 3
        self.dnext = self.n_sw
        self.dnext_sw = 0
        self.same_sync = same_sync
        self.n_inst = 0
        self.n_wait = 0
        self.pe_open = False

    def _handle(self, key):
        return self.sem[key[1]] if key[0] == "e" else self.dsem[key[1]]

    def _wait(self, engname, deps):
        need = {}
        for key, val in deps:
            if key[0] == "e" and key[1] == engname:
                if engname in ("pe", "sp") or not self.same_sync:
                    continue
            if need.get(key, 0) < val:
                need[key] = val
        seen = self.seen[engname]
        for key, val in need.items():
            if seen.get(key, 0) >= val:
                continue
            if DBG and 395 <= self.n_inst <= 404:
                print("  WAIT at inst", self.n_inst, engname, key, val, "cnt", dict(self.cnt), flush=True)
            self.eng[engname].wait_ge(self._handle(key), val)
            self.n_wait += 1
            seen[key] = val

    @staticmethod
    def _deps(reads, writes):
        deps = []
        for b in reads:
            if b.w is not None:
                deps.append(b.w)
        for b in writes:
            if b.w is not None:
                deps.append(b.w)
            deps.extend(b.r.items())
        return deps

    @staticmethod
    def _commit(key, val, reads, writes):
        for b in reads:
            if b.r.get(key, 0) < val:
                b.r[key] = val
        for b in writes:
            b.w = (key, val)
            b.r = {}

    @staticmethod
    def _bufs(xs):
        out = []
        for x in xs:
            if x is None or isinstance(x, (int, float)):
                continue
            b = x.buf if isinstance(x, Tile) else x
            if b not in out:
                out.append(b)
        return out

    def op(self, engname, emit, reads, writes, inc=True):
        if LIMIT is not None and self.n_inst >= LIMIT:
            if engname == "pe" and not inc:
                return
            if engname == "pe" and self.pe_open:
                pass
            else:
                return
        if engname == "pe":
            self.pe_open = not inc
        reads = self._bufs(reads)
        writes = self._bufs(writes)
        if engname != "pe":
            xr = [b for b in reads if b.excl]
            if xr:
                reads = [b for b in reads if not b.excl]
                writes = writes + [b for b in xr if b not in writes]
        self._wait(engname, self._deps(reads, writes))
        if TRACE_RANGE and TRACE_RANGE[0] <= self.n_inst < TRACE_RANGE[1]:
            import traceback
            fr = traceback.extract_stack(limit=4)
            print("  INST", self.n_inst, engname, [b.name for b in reads], "->", [b.name for b in writes],
                  "lines", [f.lineno for f in fr[:-1]], flush=True)
        ins = emit(self.eng[engname])
        self.n_inst += 1
        if inc:
            self.cnt[engname] += 1
            ins.then_inc(self.sem[engname], 1)
            self._commit(("e", engname), self.cnt[engname], reads, writes)
        else:
            self._commit(("e", engname), self.cnt[engname] + 1, reads, writes)

    def dma(self, qname, out, in_, extra_reads=(), extra_writes=(), slow=False):
        if LIMIT is not None and self.n_inst >= LIMIT:
            return
        if qname == "pool":
            j = self.dnext_sw
            self.dnext_sw = (j + 1) % self.n_sw
        else:
            j = self.dnext
            self.dnext = j + 1 if j + 1 < len(self.dsem) else self.n_sw
        reads = self._bufs([in_] + list(extra_reads))
        writes = self._bufs([out] + list(extra_writes))
        deps = self._deps(reads, writes)
        if self.dcnt[j] > 0:
            deps.append((("d", j), self.dcnt[j]))
        self._wait(qname, deps)
        kw = {"allow_slow_non_contiguous": True} if slow else {}
        self.eng[qname].dma_start(out=_ap(out), in_=_ap(in_), **kw).then_inc(self.dsem[j], 16)
        self.n_inst += 1
        self.dcnt[j] += 16
        self._commit(("d", j), self.dcnt[j], reads, writes)

    def barrier(self):
        for e in self.eng:
            deps = [(("e", k), v) for k, v in self.cnt.items() if v > 0]
            deps += [(("d", j), v) for j, v in enumerate(self.dcnt) if v > 0]
            self._wait(e, deps)

    def finish(self):
        deps = [(("d", j), v) for j, v in enumerate(self.dcnt) if v > 0]
        deps += [(("e", k), v) for k, v in self.cnt.items() if v > 0 and k != "sp"]
        self._wait("sp", deps)

    def mm(self, out, lhsT, rhs, start, stop, last=None):
        if last is None:
            last = stop
        self.op("pe", lambda e: e.matmul(_ap(out), _ap(lhsT), _ap(rhs), start=start, stop=stop),
                [lhsT, rhs], [out], inc=last)

    def tr(self, out, in_, ident, last=True):
        self.op("pe", lambda e: e.transpose(_ap(out), _ap(in_), _ap(ident)),
                [in_, ident], [out], inc=last)

    def act(self, out, in_, func, bias=None, scale=1.0, accum=None, eng="act"):
        kw = {}
        if bias is not None:
            kw["bias"] = _ap(bias)
        if accum is not None:
            kw["accum_out"] = _ap(accum)
        self.op("act", lambda e: e.activation(out=_ap(out), in_=_ap(in_), func=func,
                                              scale=_ap(scale), **kw),
                [in_, bias, scale], [out, accum])

    def tt(self, eng, out, in0, in1, op):
        self.op(eng, lambda e: e.tensor_tensor(out=_ap(out), in0=_ap(in0), in1=_ap(in1), op=op),
                [in0, in1], [out])

    def ts(self, eng, out, in0, s1, op0, s2=None, op1=None):
        if s2 is None:
            self.op(eng, lambda e: e.tensor_scalar(out=_ap(out), in0=_ap(in0), scalar1=_ap(s1),
                                                   scalar2=None, op0=op0),
                    [in0, s1], [out])
        else:
            self.op(eng, lambda e: e.tensor_scalar(out=_ap(out), in0=_ap(in0), scalar1=_ap(s1),
                                                   scalar2=_ap(s2), op0=op0, op1=op1),
                    [in0, s1, s2], [out])

    def stt(self, eng, out, in0, scalar, in1, op0, op1):
        self.op(eng, lambda e: e.scalar_tensor_tensor(out=_ap(out), in0=_ap(in0), scalar=_ap(scalar),
                                                      in1=_ap(in1), op0=op0, op1=op1),
                [in0, scalar, in1], [out])

    def copy(self, eng, out, in_):
        if eng == "act":
            self.op("act", lambda e: e.copy(out=_ap(out), in_=_ap(in_)), [in_], [out])
        else:
            self.op(eng, lambda e: e.tensor_copy(out=_ap(out), in_=_ap(in_)), [in_], [out])

    def scan(self, out, d0, d1, init, op0, op1):
        self.op("dve", lambda e: e.tensor_tensor_scan(out=_ap(out), data0=_ap(d0), data1=_ap(d1),
                                                      initial=init, op0=op0, op1=op1),
                [d0, d1], [out])

    def memset(self, eng, out, val):
        self.op(eng, lambda e: e.memset(_ap(out), val), [], [out])

    def recip(self, out, in_):
        self.op("dve", lambda e: e.reciprocal(out=_ap(out), in_=_ap(in_)), [in_], [out])

    def rmax(self, out, in_):
        self.op("dve", lambda e: e.reduce_max(out=_ap(out), in_=_ap(in_), axis=AX.X), [in_], [out])


class Ring:
    def __init__(self, tiles):
        self.tiles = tiles
        self.i = 0

    def get(self):
        t = self.tiles[self.i]
        self.i = (self.i + 1) % len(self.tiles)
        return t


class Prog:
    def __init__(self, T=4096, layers=(0, 1, 2, 3), TT=256, with_sample=True):
        self.T = T
        self.layers = tuple(layers)
        self.TT = TT
        self.NS = TT // 128
        self.with_sample = with_sample
        self.nc = bass.Bass("TRN2", target_bir_lowering=False)
        self.ncopy = 0

    def din(self, name, shape, dt=F32):
        return self.nc.dram_tensor(name, list(shape), dt, kind="ExternalInput").ap()

    def dout(self, name, shape, dt=F32):
        return self.nc.dram_tensor(name, list(shape), dt, kind="ExternalOutput").ap()

    def dint(self, name, shape, dt):
        return self.nc.dram_tensor(name, list(shape), dt, kind="Internal").ap()

    def sb(self, es, name, shape, dt):
        t = es.enter_context(self.nc.sbuf_tensor(name, list(shape), dt))
        return Tile(t[:] if len(shape) == 2 else t[(slice(None),) * len(shape)], name=name)

    def ps(self, es, name, shape, dt):
        t = es.enter_context(self.nc.psum_tensor(name, list(shape), dt))
        return Tile(t[(slice(None),) * len(shape)], buf=Buf(name, excl=True))

    def load_fm(self, es, name, src_rows, n, key, ptile):
        S = self.S
        tmp = self.sb(es, name + "_r", [n, 128], F32)
        S.dma("sp", tmp, Tile(src_rows, self.dbuf(key)))
        S.tr(ptile[:, 0:n], tmp, self.ident_f[0:n, 0:n])
        out = self.sb(es, name, [128, n], F32)
        S.copy("dve", out, ptile[:, 0:n])
        return out

    def evac(self, out, in_):
        self.ncopy += 1
        self.S.copy("act" if self.ncopy % 2 else "dve", out, in_)

    def build(self):
        nc = self.nc
        T = self.T
        with ExitStack() as es:
            self.S = S = Sched(nc, es)
            self._declare_io()
            self._consts(es)
            jobs0 = self._prep_jobs(self.layers[0])
            if self.layers[0] in (0, 3):
                first_use = []
                for g in range(4):
                    first_use += [g, 4 + g, 8 + 2 * g, 9 + 2 * g, 16 + 2 * g, 17 + 2 * g]
                jobs0 = [jobs0[i] for i in first_use] + jobs0[24:]
            for job in jobs0:
                job()
            self.xs_t = self.sb(es, "xs_t", [128, D], F32)
            if self.with_sample:
                S.memset("pool", self.xs_t, 0.0)
                S.dma("sp", self.xs_t[0:DEC, :], Tile(self.io["xs"], self.dbuf("xs")))
            n_gla = n_hgrn = n_diff = 0
            for li in range(4):
                kind = li % 3
                idx = (n_gla, n_hgrn, n_diff)[kind]
                if li in self.layers:
                    first = li == self.layers[0]
                    last = li == self.layers[-1]
                    if kind == 0:
                        self.layer_lin(li, "gla", idx, first, last)
                    elif kind == 1:
                        self.layer_lin(li, "hgrn", idx, first, last)
                    else:
                        self.layer_diff(li, idx, first, last)
                    S.barrier()
                if kind == 0:
                    n_gla += 1
                elif kind == 1:
                    n_hgrn += 1
                else:
                    n_diff += 1
            S.finish()
        return nc

    def dbuf(self, key):
        d = self.__dict__.setdefault("_dbufs", {})
        if key not in d:
            d[key] = Buf(key)
        return d[key]

    def _declare_io(self):
        T = self.T
        io = {}
        io["xp"] = self.din("xp", [T, D])
        io["xs"] = self.din("xs", [DEC, D])
        io["sg"] = self.din("sg", [2, 4, 256, 512])
        io["sh"] = self.din("sh", [1, 16, 128, 128])
        io["ck"] = self.din("ck", [PAST, D])
        io["cv"] = self.din("cv", [PAST, D])
        io["norm_w"] = self.din("norm_w", [4, D])
        io["final_norm_w"] = self.din("final_norm_w", [D])
        io["gla_w_in"] = self.din("gla_w_in", [2, D, 6144])
        io["gla_w_a1"] = self.din("gla_w_a1", [2, D, 16])
        io["gla_w_a2"] = self.din("gla_w_a2", [2, 16, 1024])
        io["gla_b_a"] = self.din("gla_b_a", [2, 1024])
        io["gla_norm_w"] = self.din("gla_norm_w", [2, 512])
        io["gla_w_out"] = self.din("gla_w_out", [2, D, D])
        io["hgrn_w_in"] = self.din("hgrn_w_in", [1, D, 8192])
        io["hgrn_lower_bounds"] = self.din("hgrn_lower_bounds", [4, D])
        io["hgrn_norm_w"] = self.din("hgrn_norm_w", [1, D])
        io["hgrn_w_out"] = self.din("hgrn_w_out", [1, D, D])
        io["diff_w_in"] = self.din("diff_w_in", [1, D, 8192])
        io["diff_lambda"] = self.din("diff_lambda", [1, 4, 128])
        io["diff_subln_w"] = self.din("diff_subln_w", [1, 256])
        io["diff_w_out"] = self.din("diff_w_out", [1, D, D])
        io["consts"] = self.din("consts", [128, 512])
        io["yp"] = self.dout("yp", [T, D])
        io["ys"] = self.dout("ys", [DEC, D])
        io["gsp"] = self.dout("gsp", [2, 4, 256, 512])
        io["gss"] = self.dout("gss", [2, 4, 256, 512])
        io["hsp"] = self.dout("hsp", [1, 16, 128, 128])
        io["hss"] = self.dout("hss", [1, 16, 128, 128])
        io["krp"] = self.dout("krp", [T, D])
        io["vrp"] = self.dout("vrp", [T, D])
        io["krs"] = self.dout("krs", [DEC, D])
        io["vrs"] = self.dout("vrs", [DEC, D])
        io["xres"] = self.dint("xres", [T, D], F32)
        io["wb_gla_in"] = self.dint("wb_gla_in", [2, 24, 128, KC * 256], BF16)
        io["wb_gla_out"] = self.dint("wb_gla_out", [2, 8, 128, KC * 256], BF16)
        io["wb_hgrn_in"] = self.dint("wb_hgrn_in", [1, 32, 128, KC * 256], BF16)
        io["wb_hgrn_out"] = self.dint("wb_hgrn_out", [1, 8, 128, KC * 256], BF16)
        io["wb_diff_in"] = self.dint("wb_diff_in", [1, 32, 128, KC * 256], BF16)
        io["wb_diff_out"] = self.dint("wb_diff_out", [1, 8, 128, KC * 256], BF16)
        self.io = io

    def _consts(self, es):
        S = self.S
        c = self.sb(es, "consts_f", [128, 512], F32)
        S.dma("sp", c, Tile(self.io["consts"], self.dbuf("consts")))
        self.ident_f = c[:, 0:128]
        self.mask_f = c[:, 128:256]
        self.ones_f = c[:, 256:384]
        self.ident = self.sb(es, "ident_b", [128, 128], BF16)
        S.copy("dve", self.ident, self.ident_f)
        self.eps_t = self.sb(es, "eps_t", [128, 1], F32)
        S.memset("pool", self.eps_t, EPS)

    def _prep_jobs(self, li):
        S = self.S
        io = self.io
        jobs = {0: [("gla_w_in", 0), ("gla_w_out", 0)], 1: [("hgrn_w_in", 0), ("hgrn_w_out", 0)],
                2: [("diff_w_in", 0), ("diff_w_out", 0)], 3: [("gla_w_in", 1), ("gla_w_out", 1)]}[li]
        out = []
        for name, j in jobs:
            src = io[name]
            dst = io["wb_" + name.replace("_w_", "_")]
            ncols = src.shape[2]
            for blk in range(ncols // 256):
                def job(src=src, dst=dst, name=name, j=j, blk=blk):
                    S.dma("pool", Tile(dst[j, blk].rearrange("p (kc c) -> p kc c", kc=KC), self.dbuf(("wb", name, j, blk))),
                          Tile(src[j, :, blk * 256:(blk + 1) * 256].rearrange("(kc p) c -> p kc c", p=128), self.dbuf(("w", name))))
                out.append(job)
        return out

    def prep_next(self, li):
        later = [l for l in self.layers if l > li]
        if later:
            self.prep_q = self.__dict__.get("prep_q", []) + self._prep_jobs(later[0])

    def prep_some(self, n):
        q = self.__dict__.get("prep_q", [])
        for _ in range(min(n, len(q))):
            q.pop(0)()

    def stage1_gen(self, L, xt_all, hT, TT, NS):
        S = self.S
        for j in range(NS):
            xt = xt_all[:, j, :]
            ss = L["ss_ring"].get()
            xn = L["xn_ring"].get()
            S.act(xn, xt, AF.Square, accum=ss[:, 0:1])
            S.act(ss[:, 1:2], ss[:, 0:1], AF.Ln, bias=self.eps_t, scale=1.0 / D)
            S.act(ss[:, 2:3], ss[:, 1:2], AF.Exp, scale=-0.5)
            S.act(xn, xt, AF.Copy, scale=ss[:, 2:3])
            yield
            for q in range(4):
                pt = L["ptr_ring"].get()
                for i in range(4):
                    kc = 4 * q + i
                    S.tr(pt[:, i, :], xn[:, kc * 128:(kc + 1) * 128], self.ident, last=(i == 3))
                S.tt("dve", hT[:, 4 * q:4 * q + 4, j * 128:(j + 1) * 128], pt[:, 0:4, :],
                     L["nw_fm"][:, 4 * q:4 * q + 4].unsq(2).bc([128, 4, 128]), ALU.mult)
                yield

    def stage1(self, L, xt_all, hT, TT, NS):
        for _ in self.stage1_gen(L, xt_all, hT, TT, NS):
            pass

    def wload(self, L, wsrc, key, c0, ncols=256):
        wt = L["wring"].get()
        blk = c0 // 256
        self.S.dma("sp", wt.re("p kc c -> p (kc c)"), Tile(wsrc[blk], self.dbuf(tuple(key) + (blk,))))
        return wt

    def fm_proj(self, L, wt, c, hT, TT, width=128):
        pp = L["pproj_ring"].get()
        for kc in range(KC):
            self.S.mm(pp[0:width, 0:TT], wt[:, kc, c * 128:c * 128 + width], hT[:, kc, 0:TT],
                      start=(kc == 0), stop=(kc == KC - 1))
        return pp

    def tm_proj(self, L, wt, hT, j, ncols=256):
        pp = L["pproj_ring"].get()
        for kc in range(KC):
            self.S.mm(pp[:, 0:ncols], hT[:, kc, j * 128:(j + 1) * 128], wt[:, kc, 0:ncols],
                      start=(kc == 0), stop=(kc == KC - 1))
        return pp

    def layer_lin(self, li, kind, idx, first, last):
        S = self.S
        io = self.io
        T, TT, NS = self.T, self.TT, self.NS
        gla = kind == "gla"
        KCg = 2 if gla else 4
        HG = 1 if gla else 4
        dkc = 2 if gla else 1
        dvh = 512 if gla else 128
        KCT = 8 if gla else 16
        qscale = (256.0 if gla else 128.0) ** -0.5
        gsc = (1.0 / 16.0) if gla else -1.0
        if gla:
            w_in = io["wb_gla_in"][idx]
            w_out = io["wb_gla_out"][idx]
            qoff, koff, voff, zoff = 0, 1024, 2048, 4096
            kin, kout = ("wb", "gla_w_in", idx), ("wb", "gla_w_out", idx)
        else:
            w_in = io["wb_hgrn_in"][idx]
            w_out = io["wb_hgrn_out"][idx]
            qoff, koff, voff, zoff = 0, 2048, 4096, 6144
            kin, kout = ("wb", "hgrn_w_in", idx), ("wb", "hgrn_w_out", idx)
        with ExitStack() as es:
            L = {}
            sb = lambda n, s, d: self.sb(es, "l%d_%s" % (li, n), s, d)
            psb = lambda n, s, d: self.ps(es, "l%d_%s" % (li, n), s, d)
            xt_all2 = [sb("xt%d" % i, [128, NS, D], F32) for i in range(2)]
            hT2 = [sb("hT%d" % i, [128, KC, TT], BF16) for i in range(2)]
            gT2 = [sb("gT%d" % i, [128, KC, TT], BF16) for i in range(2)]
            L["xn_ring"] = Ring([sb("xn%d" % i, [128, D], BF16) for i in range(1 if not gla else 2)])
            L["junk"] = None
            L["ss_ring"] = Ring([sb("ss%d" % i, [128, 4], F32) for i in range(4)])
            L["wring"] = Ring([sb("w%d" % i, [128, KC, 256], BF16) for i in range(3)])
            L["pproj_ring"] = Ring([psb("pp%d" % i, [128, 512], F32) for i in range(3)])
            L["ptr_ring"] = Ring([psb("pt%d" % i, [128, 8, 128], BF16) for i in range(2)])
            pch = Ring([psb("pc%d" % i, [128, 512], F32) for i in range(3)])
            nw_fm = self.load_fm(es, "l%d_nw_fm" % li, io["norm_w"][li].rearrange("(c p) -> c p", p=128), KC,
                                 "norm_w", L["pproj_ring"].get())
            L["nw_fm"] = nw_fm
            nw_b = sb("nw_b", [128, D], F32)
            if gla:
                for h in range(4):
                    S.dma("sp", nw_b[:, h * 512:(h + 1) * 512],
                          Tile(io["gla_norm_w"][idx].partition_broadcast(128), self.dbuf("gla_norm_w")))
            else:
                S.dma("sp", nw_b, Tile(io["hgrn_norm_w"][idx].partition_broadcast(128), self.dbuf("hgrn_norm_w")))
            fnw_b = None
            if last:
                fnw_b = sb("fnw_b", [128, D], F32)
                S.dma("sp", fnw_b, Tile(io["final_norm_w"].partition_broadcast(128), self.dbuf("final_norm_w")))
            Sst = sb("S", [128, KCT, dvh], F32)
            Sbf = sb("Sbf", [128, KCT, dvh], BF16)
            grp = []
            for i in range(2):
                grp.append(dict(
                    qT=sb("qT%d" % i, [128, KCg, TT], BF16),
                    kT=sb("kT%d" % i, [128, KCg, TT], BF16),
                    lg=sb("lg%d" % i, [128, KCg, TT], F32),
                    v=sb("v%d" % i, [128, NS, 512], BF16),
                    zs=sb("zs%d" % i, [128, NS, 512], BF16),
                ))
            tmp = []
            for i in range(2):
                tmp.append(dict(
                    Bl=sb("Bl%d" % i, [128, KCg, 128], F32),
                    Bm=sb("Bm%d" % i, [128, KCg, 128], F32),
                    Ei=sb("Ei%d" % i, [128, KCg, 128], F32),
                    Ea=sb("Ea%d" % i, [128, KCg, 128], F32),
                    Ek=sb("Ek%d" % i, [128, KCg, 128], F32),
                    Es=sb("Es%d" % i, [128, KCg, 128], F32),
                    qi=sb("qi%d" % i, [128, KCg, 128], BF16),
                    qa=sb("qa%d" % i, [128, KCg, 128], BF16),
                    ka=sb("ka%d" % i, [128, KCg, 128], BF16),
                    ksT=sb("ksT%d" % i, [128, KCg, 128], BF16),
                    ks=sb("ks%d" % i, [128, KCg, 128], BF16),
                    atT=sb("atT%d" % i, [128, HG, 128], BF16),
                    nwz=sb("nwz%d" % i, [128, 512], F32),
                    gt=sb("gt%d" % i, [128, 512], BF16),
                    dS=sb("dS%d" % i, [128, KCg, 1], F32),
                    osb=sb("osb%d" % i, [128, 512], F32),
                ))
            ssg2 = [sb("ssg%d" % i, [128, NS, 8], F32) for i in range(2)]
            lnq = sb("lnq", [128, 1], F32)
            S.memset("pool", lnq, math.log(qscale))
            rstd_tok2 = [sb("rstd_tok%d" % i, [128, NS, 2], F32) for i in range(2)]
            if gla:
                w_a1 = sb("w_a1", [128, KC, 16], BF16)
                S.dma("pool", w_a1, Tile(io["gla_w_a1"][idx].rearrange("(kc p) r -> p kc r", p=128),
                                         self.dbuf("gla_w_a1")))
                w_a2 = sb("w_a2", [32, 1024], F32)
                S.dma("sp", w_a2[0:16, :], Tile(io["gla_w_a2"][idx], self.dbuf("gla_w_a2")))
                S.dma("sp", w_a2[16:17, :], Tile(io["gla_b_a"][idx].unsqueeze(0), self.dbuf("gla_b_a")))
                raug2 = [sb("raug%d" % i, [32, TT], F32) for i in range(2)]
                for r_ in raug2:
                    S.memset("pool", r_, 1.0)
            else:
                lbp = self.load_fm(es, "l%d_lbp" % li, io["hgrn_lower_bounds"].rearrange("l (c p) -> (l c) p", p=128),
                                   4 * KC, "hgrn_lb", L["pproj_ring"].get()).re("p (l c) -> p l c", l=4)
                lbe = sb("lbe", [128, 4, KC], F32)
                S.act(lbe, lbp, AF.Exp)
                lbz = sb("lbz", [128, KC], F32)
                S.tt("dve", lbz, lbe[:, 0, :], lbe[:, 1, :], ALU.add)
                S.tt("dve", lbz, lbz, lbe[:, 2, :], ALU.add)
                S.tt("dve", lbz, lbz, lbe[:, 3, :], ALU.add)
                lbn = sb("lbn", [128, KC], F32)
                S.memset("pool", lbn, 0.0)
                for jj in range(1, li + 1):
                    S.tt("dve", lbn, lbn, lbe[:, jj, :], ALU.add)
                S.recip(lbz, lbz)
                S.tt("dve", lbn, lbn, lbz, ALU.mult)
                oml = sb("oml", [128, KC], F32)
                S.ts("dve", oml, lbn, -1.0, ALU.mult, 1.0, ALU.add)
                kf = [sb("kf%d" % i, [128, TT], F32) for i in range(2)]

            print("layer", li, kind, "sbuf bytes remaining", self.nc.sbuf_bytes_remaining, flush=True)
            def tile_A(c):
                x_src, TTc, NSc, valid = c["x_src"], c["TTc"], c["NSc"], c["valid"]
                xt_all, hT = xt_all2[c["p"]], hT2[c["p"]]
                if gla:
                    raug = raug2[c["p"]]
                if valid < 128:
                    S.copy("pool", xt_all[:, 0, :], self.xs_t)
                else:
                    for j in range(NSc):
                        S.dma("sp", xt_all[:, j, :], x_src(j))
                yield
                yield from self.stage1_gen(L, xt_all, hT, TTc, NSc)
                if gla:
                    pp = L["pproj_ring"].get()
                    for kc in range(KC):
                        S.mm(pp[0:16, 0:TTc], w_a1[:, kc, :], hT[:, kc, 0:TTc],
                             start=(kc == 0), stop=(kc == KC - 1))
                    S.copy("dve", raug[0:16, 0:TTc], pp[0:16, 0:TTc])
                yield

            def tile_B(c, g):
                    TTc, NSc, valid = c["TTc"], c["NSc"], c["valid"]
                    hT = hT2[c["p"]]
                    if gla:
                        raug = raug2[c["p"]]
                    G = grp[g % 2]
                    qT, kT, lg, v, zs = G["qT"], G["kT"], G["lg"], G["v"], G["zs"]
                    nblk = KCg // 2
                    for b in range(nblk):
                        wt = self.wload(L, w_in, kin, qoff + g * KCg * 128 + b * 256)
                        for c in range(2):
                            pp = self.fm_proj(L, wt, c, hT, TTc)
                            if gla:
                                self.evac(qT[:, 2 * b + c, 0:TTc], pp[:, 0:TTc])
                            else:
                                S.act(qT[:, 2 * b + c, 0:TTc], pp[:, 0:TTc], AF.Silu)
                            yield
                    for b in range(nblk):
                        wt = self.wload(L, w_in, kin, koff + g * KCg * 128 + b * 256)
                        for c in range(2):
                            kc = 2 * b + c
                            pp = self.fm_proj(L, wt, c, hT, TTc)
                            if gla:
                                self.evac(kT[:, kc, 0:TTc], pp[:, 0:TTc])
                            else:
                                kcg = g * KCg + kc
                                kff = kf[kc % 2]
                                S.act(kff[:, 0:TTc], pp[:, 0:TTc], AF.Sigmoid, scale=-1.0)
                                S.ts("dve", kff[:, 0:TTc], kff[:, 0:TTc], oml[:, kcg:kcg + 1], ALU.mult)
                                S.copy("pool", kT[:, kc, 0:TTc], kff[:, 0:TTc])
                                S.act(lg[:, kc, 0:TTc], kff[:, 0:TTc], AF.Ln, bias=1.0, scale=-1.0)
                            yield
                    if gla:
                        for kc in range(KCg):
                            kcg = g * KCg + kc
                            pp = L["pproj_ring"].get()
                            S.mm(pp[:, 0:TTc], w_a2[0:17, kcg * 128:(kcg + 1) * 128], raug[0:17, 0:TTc],
                                 start=True, stop=True)
                            S.act(lg[:, kc, 0:TTc], pp[:, 0:TTc], AF.Exp, scale=-1.0)
                            S.act(lg[:, kc, 0:TTc], lg[:, kc, 0:TTc], AF.Ln, bias=1.0)
                            yield
                    if valid < 128:
                        S.memset("pool", lg[:, :, valid:128], 0.0)
                    for b in range(2):
                        wt = self.wload(L, w_in, kin, voff + g * 512 + b * 256)
                        for j in range(NSc):
                            pp = self.tm_proj(L, wt, hT, j)
                            self.evac(v[:, j, b * 256:(b + 1) * 256], pp[:, 0:256])
                            yield
                    for b in range(2):
                        wt = self.wload(L, w_in, kin, zoff + g * 512 + b * 256)
                        for j in range(NSc):
                            pp = self.tm_proj(L, wt, hT, j)
                            S.act(zs[:, j, b * 256:(b + 1) * 256], pp[:, 0:256], AF.Silu)
                            yield

            def tile_C(c, g):
                    TTc, NSc, valid = c["TTc"], c["NSc"], c["valid"]
                    gT, ssg = gT2[c["p"]], ssg2[c["p"]]
                    G = grp[g % 2]
                    qT, kT, lg, v, zs = G["qT"], G["kT"], G["lg"], G["v"], G["zs"]
                    gens = [chunk(c, g, j) for j in range(NSc)]
                    if NSc == 1:
                        yield from gens[0]
                        return
                    ga, gb = gens
                    for k in range(3):
                        next(ga)
                        yield
                        next(gb)
                        yield
                    next(ga)
                    yield
                    next(ga)
                    yield
                    next(gb)
                    yield
                    next(gb)
                    yield
                    for _ in ga:
                        pass
                    for _ in gb:
                        pass

            def chunk(c, g, j):
                    TTc, NSc, valid = c["TTc"], c["NSc"], c["valid"]
                    gT, ssg = gT2[c["p"]], ssg2[c["p"]]
                    G = grp[g % 2]
                    qT, kT, lg, v, zs = G["qT"], G["kT"], G["lg"], G["v"], G["zs"]
                    if True:
                        W = tmp[j % 2]
                        cs = slice(j * 128, (j + 1) * 128)
                        Bl, Bm = W["Bl"], W["Bm"]
                        for kc in range(KCg):
                            S.scan(Bl[:, kc, :], self.ones_f, lg[:, kc, cs], 0.0, ALU.mult, ALU.add)
                        S.tt("dve", Bm, Bl, Bl[:, :, 63:64].bc([128, KCg, 128]), ALU.subtract)
                        S.act(W["Ea"], Bm, AF.Exp, scale=-gsc, bias=lnq)
                        S.act(W["Ek"], Bm, AF.Exp, scale=gsc)
                        S.act(W["Ei"], Bl, AF.Exp, scale=-gsc, bias=lnq)
                        S.act(W["dS"], Bl[:, :, 127:128], AF.Exp, scale=-gsc)
                        Bs = Bm
                        S.tt("pool", Bs, Bl, Bl[:, :, 127:128].bc([128, KCg, 128]), ALU.subtract)
                        S.act(W["Es"], Bs, AF.Exp, scale=gsc)
                        S.tt("dve", W["qi"], qT[:, :, cs], W["Ei"], ALU.mult)
                        S.tt("pool", W["qa"], qT[:, :, cs], W["Ea"], ALU.mult)
                        S.tt("dve", W["ka"], kT[:, :, cs], W["Ek"], ALU.mult)
                        S.tt("pool", W["ksT"], kT[:, :, cs], W["Es"], ALU.mult)
                        yield
                        pt = L["ptr_ring"].get()
                        for kc in range(KCg):
                            S.tr(pt[:, kc, :], W["ksT"][:, kc, :], self.ident, last=(kc == KCg - 1))
                        self.evac(W["ks"], pt[:, 0:KCg, :])
                        yield
                        pa = pch.get()
                        for h in range(HG if "noA" not in DBG else 1):
                            for c in range(dkc):
                                kc = h * dkc + c
                                S.mm(pa[:, h * 128:(h + 1) * 128], W["ka"][:, kc, :], W["qa"][:, kc, :],
                                     start=(c == 0), stop=(c == dkc - 1), last=(c == dkc - 1))
                        if "noM" in DBG:
                            for h in range(HG):
                                S.tt("dve", W["atT"][:, h, :], pa[:, h * 128:(h + 1) * 128], self.mask_f, ALU.mult)
                        else:
                            S.tt("dve", W["atT"], pa[:, 0:HG * 128].re("p (h t) -> p h t", h=HG),
                                 self.mask_f.unsq(1).bc([128, HG, 128]), ALU.mult)
                        yield
                        po = pch.get()
                        for h in range(HG if "noO" not in DBG else 1):
                            vs = slice(h * dvh, (h + 1) * dvh)
                            S.mm(po[:, vs], W["atT"][:, h, :], v[:, j, vs], start=True, stop=False, last=False)
                            for c in range(dkc):
                                kc = h * dkc + c
                                S.mm(po[:, vs], W["qi"][:, kc, :], Sbf[:, g * KCg + kc, :],
                                     start=False, stop=(c == dkc - 1), last=(c == dkc - 1))
                        gcols = slice(g * 512, (g + 1) * 512)
                        S.tt("pool", W["nwz"], zs[:, j, :], nw_b[:, gcols], ALU.mult)
                        osb = W["osb"]
                        S.copy("act", osb, po[:, 0:512])
                        S.act(W["gt"], osb, AF.Square, accum=ssg[:, j, g:g + 1])
                        if gla:
                            S.act(ssg[:, j, 4 + g:5 + g], ssg[:, j, g:g + 1], AF.Ln, bias=self.eps_t, scale=1.0 / 512)
                            S.act(ssg[:, j, 4 + g:5 + g], ssg[:, j, 4 + g:5 + g], AF.Exp, scale=-0.5)
                            S.stt("dve", W["gt"], osb, ssg[:, j, 4 + g:5 + g], W["nwz"], ALU.mult, ALU.mult)
                        else:
                            S.tt("dve", W["gt"], osb, W["nwz"], ALU.mult)
                        yield
                        pt = L["ptr_ring"].get()
                        for c in range(4):
                            S.tr(pt[:, c, :], W["gt"][:, c * 128:(c + 1) * 128], self.ident, last=(c == 3))
                        self.evac(gT[:, g * 4:(g + 1) * 4, cs], pt[:, 0:4, :])
                        for h in range(HG if "noS" not in DBG else 0):
                            vs = slice(h * dvh, (h + 1) * dvh)
                            for c in range(dkc):
                                kc = h * dkc + c
                                pu = pch.get()
                                S.mm(pu[:, 0:dvh], W["ks"][:, kc, :], v[:, j, vs], start=True, stop=True)
                                kcg = g * KCg + kc
                                S.stt("dve", Sst[:, kcg, :], Sst[:, kcg, :], W["dS"][:, kc, :],
                                      pu[:, 0:dvh], ALU.mult, ALU.add)
                                S.copy("pool", Sbf[:, kcg, :], Sst[:, kcg, :])
                        yield

            def tile_D(c):
                x_dst, y_dst, TTc, NSc, valid = c["x_dst"], c["y_dst"], c["TTc"], c["NSc"], c["valid"]
                xt_all, gT, ssg, rstd_tok = xt_all2[c["p"]], gT2[c["p"]], ssg2[c["p"]], rstd_tok2[c["p"]]
                if not gla:
                    for j in range(NSc):
                        S.tt("dve", ssg[:, j, 4:5], ssg[:, j, 0:1], ssg[:, j, 1:2], ALU.add)
                        S.tt("dve", ssg[:, j, 5:6], ssg[:, j, 2:3], ssg[:, j, 3:4], ALU.add)
                        S.tt("dve", ssg[:, j, 4:5], ssg[:, j, 4:5], ssg[:, j, 5:6], ALU.add)
                        S.act(rstd_tok[:, j, 0:1], ssg[:, j, 4:5], AF.Ln, bias=self.eps_t, scale=1.0 / D)
                        S.act(rstd_tok[:, j, 1:2], rstd_tok[:, j, 0:1], AF.Exp, scale=-0.5)
                for cb in range(8 if "noout" not in DBG else 0):
                    wt = self.wload(L, w_out, kout, cb * 256)
                    cols = slice(cb * 256, (cb + 1) * 256)
                    for j in range(NSc):
                        pp = L["pproj_ring"].get()
                        for ic in range(KC):
                            S.mm(pp[:, 0:256], gT[:, ic, j * 128:(j + 1) * 128], wt[:, ic, :],
                                 start=(ic == 0), stop=(ic == KC - 1))
                        if gla:
                            S.tt("dve", xt_all[:, j, cols], xt_all[:, j, cols], pp[:, 0:256], ALU.add)
                        else:
                            S.stt("dve", xt_all[:, j, cols], pp[:, 0:256], rstd_tok[:, j, 1:2],
                                  xt_all[:, j, cols], ALU.mult, ALU.add)
                        yield
                for j in range(NSc):
                    if valid < 128:
                        S.copy("pool", self.xs_t, xt_all[:, 0, :])
                    elif not last:
                        S.dma("sp", x_dst(j), xt_all[:, j, :])
                    if last:
                        self.final_norm(L, xt_all[:, j, :], fnw_b, y_dst(j), valid)


            def exhaust(gen):
                for _ in gen:
                    pass

            def merge(main, side, n_main, n_side):
                main, side = iter(main), iter(side)
                done_side = 0
                i = 0
                for _ in main:
                    i += 1
                    want = (i * n_side + n_main - 1) // n_main
                    while done_side < want:
                        done_side += 1
                        if next(side, "END") == "END":
                            done_side = 10 ** 9
                for _ in side:
                    pass

            def chain(*gens):
                for g_ in gens:
                    yield from g_

            def run_pipeline(ctxs, do_prep=False):
                n = len(ctxs)
                for i, c in enumerate(ctxs):
                    c["p"] = i % 2
                nB_ = lambda c: (6 if gla else 8) + 4 * c["NSc"]
                nD_ = lambda c: 8 * c["NSc"]
                nC_ = lambda c: 5 * c["NSc"]
                nA_ = lambda c: 2 + 5 * c["NSc"]
                exhaust(tile_A(ctxs[0]))
                side_prev, n_prev = iter(()), 0
                for t in range(n):
                    c = ctxs[t]
                    nB, nC = nB_(c), nC_(c)
                    merge(tile_B(c, 0), side_prev, nB, n_prev)
                    if "pre_C" in c:
                        c["pre_C"]()
                    mains = chain(tile_D(ctxs[t - 1]), tile_B(c, 1)) if t > 0 else tile_B(c, 1)
                    merge(mains, tile_C(c, 0), nB + (nD_(ctxs[t - 1]) if t > 0 else 0), nC)
                    genA = tile_A(ctxs[t + 1]) if t + 1 < n else iter(())
                    nA = nA_(ctxs[t + 1]) if t + 1 < n else 0
                    hA = nA // 2
                    merge(tile_B(c, 2), chain(tile_C(c, 1), itertools.islice(genA, hA)), nB, nC + hA)
                    merge(tile_B(c, 3), chain(tile_C(c, 2), genA), nB, nC + nA - hA)
                    side_prev, n_prev = tile_C(c, 3), nC
                    if t == 0 and do_prep:
                        self.prep_next(li)
                    self.prep_some(4)
                exhaust(side_prev)
                self.prep_some(1000)
                exhaust(tile_D(ctxs[n - 1]))

            S.memset("pool", Sst, 0.0)
            S.memset("pool", Sbf, 0.0)
            xsrc_ap = io["xp"] if first else io["xres"]
            xsrc_key = "xp" if first else "xres"
            ctxs = []
            for ti in range(T // TT):
                r0 = ti * TT
                ctxs.append(dict(
                    x_src=lambda j, r0=r0: Tile(xsrc_ap[r0 + j * 128:r0 + (j + 1) * 128, :], self.dbuf((xsrc_key, r0 + j * 128))),
                    x_dst=lambda j, r0=r0: Tile(io["xres"][r0 + j * 128:r0 + (j + 1) * 128, :], self.dbuf(("xres", r0 + j * 128))),
                    y_dst=lambda j, r0=r0: Tile(io["yp"][r0 + j * 128:r0 + (j + 1) * 128, :], self.dbuf(("yp", r0 + j * 128))),
                    TTc=TT, NSc=NS, valid=128))
            def swap_state():
                if gla:
                    S.dma("sp", Tile(io["gsp"][idx].rearrange("h (c p) v -> p (h c) v", p=128), self.dbuf(("gsp", idx))), Sst)
                elif "nostate" not in DBG:
                    S.dma("sp", Tile(io["hsp"][idx].rearrange("h p v -> p h v"), self.dbuf(("hsp", idx))), Sst)
                if self.with_sample:
                    if gla:
                        S.dma("sp", Sst, Tile(io["sg"][idx].rearrange("h (c p) v -> p (h c) v", p=128), self.dbuf("sg")))
                    else:
                        S.dma("sp", Sst, Tile(io["sh"][idx].rearrange("h p v -> p h v"), self.dbuf("sh")))
                    S.copy("pool", Sbf, Sst)

            if self.with_sample:
                ctxs.append(dict(x_src=None, x_dst=None, y_dst=lambda j: Tile(io["ys"], self.dbuf("ys")),
                                 TTc=128, NSc=1, valid=DEC, pre_C=swap_state))
                run_pipeline(ctxs, do_prep=True)
                if gla:
                    S.dma("sp", Tile(io["gss"][idx].rearrange("h (c p) v -> p (h c) v", p=128), self.dbuf(("gss", idx))), Sst)
                else:
                    S.dma("sp", Tile(io["hss"][idx].rearrange("h p v -> p h v"), self.dbuf(("hss", idx))), Sst)
            else:
                run_pipeline(ctxs, do_prep=True)
                swap_state()
            S.barrier()

    def final_norm(self, L, xt, fnw_b, ydst, valid):
        S = self.S
        ss = L["ss_ring"].get()
        junk = L["junk"] if L.get("junk") is not None else L["xn_ring"].get()
        S.act(junk, xt, AF.Square, accum=ss[:, 0:1])
        S.act(ss[:, 1:2], ss[:, 0:1], AF.Ln, bias=self.eps_t, scale=1.0 / D)
        S.act(ss[:, 2:3], ss[:, 1:2], AF.Exp, scale=-0.5)
        S.stt("dve", xt, xt, ss[:, 2:3], fnw_b, ALU.mult, ALU.mult)
        if valid < 128:
            S.dma("sp", ydst, xt[0:valid, :])
        else:
            S.dma("sp", ydst, xt)

    def layer_diff(self, li, idx, first, last):
        S = self.S
        io = self.io
        T, TT, NS = self.T, self.TT, self.NS
        TP = T + 128
        NKT = T // 128
        lam_init = 0.8 - 0.6 * math.exp(-0.3 * li)
        w_in = io["wb_diff_in"][idx]
        w_out = io["wb_diff_out"][idx]
        kin, kout = ("wb", "diff_w_in", idx), ("wb", "diff_w_out", idx)
        qT_d = self.dint("qT_d", [16, 128, TP], BF16)
        kT_d = self.dint("kT_d", [16, 128, TP], BF16)
        v_d = self.dint("v_d", [TP, D], BF16)
        z_d = self.dint("z_d", [TP, D], BF16)
        o_d = self.dint("o_d", [TP, D], BF16)
        ntile = T // TT
        tb = lambda name, r0: self.dbuf((name, r0 // TT if r0 < T else ntile))
        all_tb = lambda name: [self.dbuf((name, i)) for i in range(ntile + 1)]
        with ExitStack() as es:
            sb = lambda n, s, d: self.sb(es, "l%d_%s" % (li, n), s, d)
            stat = sb("stat", [128, 64], F32)
            qmax = sb("qmax", [128, 16], F32)
            kmax = sb("kmax", [128, 16], F32)
            S.memset("pool", qmax, 0.0)
            S.memset("pool", kmax, 0.0)
            negc = sb("negc", [128, 1], F32)
            neglam = sb("neglam", [128, 1], F32)
            slw_b = sb("slw_b", [128, 256], F32)
            S.dma("sp", slw_b, Tile(io["diff_subln_w"][idx].partition_broadcast(128), self.dbuf("diff_subln_w")))
            S.act(slw_b, slw_b, AF.Copy, scale=1.0 - lam_init)
            lamb = sb("lamb", [128, 4, 128], F32)
            S.dma("sp", lamb, Tile(io["diff_lambda"][idx].rearrange("a b -> (a b)").partition_broadcast(128),
                                   self.dbuf("diff_lambda")))
            lp = sb("lp", [128, 2, 128], F32)
            S.tt("dve", lp[:, 0, :], lamb[:, 0, :], lamb[:, 1, :], ALU.mult)
            S.tt("dve", lp[:, 1, :], lamb[:, 2, :], lamb[:, 3, :], ALU.mult)
            S.op("dve", lambda e: e.reduce_sum(out=stat.ap[:, 0:2], in_=lp.ap, axis=AX.X), [lp], [stat])
            S.act(stat[:, 2:4], stat[:, 0:2], AF.Exp)
            S.tt("dve", stat[:, 4:5], stat[:, 3:4], stat[:, 2:3], ALU.subtract)
            S.ts("dve", neglam, stat[:, 4:5], -lam_init, ALU.add)
            kTc = sb("kTc", [128, 16, PAST + 128], BF16)
            vc = sb("vc", [128, PAST // 128 + 1, 8, 257], BF16)
            S.memset("pool", vc[:, :, :, 256:257], 1.0)

            with ExitStack() as ea:
                L = {}
                sa = lambda n, s, d: self.sb(ea, "l%dA_%s" % (li, n), s, d)
                psa = lambda n, s, d: self.ps(ea, "l%dA_%s" % (li, n), s, d)
                xt_all = sa("xt", [128, NS, D], F32)
                hT = sa("hT", [128, KC, TT], BF16)
                L["junk"] = sa("junk", [128, D], BF16)
                L["xn_ring"] = Ring([sa("xn%d" % i, [128, D], BF16) for i in range(2)])
                L["ss_ring"] = Ring([sa("ss%d" % i, [128, 4], F32) for i in range(4)])
                L["wring"] = Ring([sa("w%d" % i, [128, KC, 256], BF16) for i in range(3)])
                L["pproj_ring"] = Ring([psa("pp%d" % i, [128, 512], F32) for i in range(3)])
                L["ptr_ring"] = Ring([psa("pt%d" % i, [128, 8, 128], BF16) for i in range(2)])
                L["nw_fm"] = self.load_fm(ea, "l%d_nw_fm" % li, io["norm_w"][li].rearrange("(c p) -> c p", p=128), KC,
                                          "norm_w", L["pproj_ring"].get())
                qT_t = sa("qT_t", [128, 16, TT], BF16)
                kT_t = sa("kT_t", [128, 16, TT], BF16)
                v_t = sa("v_t", [128, NS, D], BF16)
                z_t = sa("z_t", [128, NS, D], BF16)
                f32r = Ring([sa("f32r%d" % i, [128, 256], F32) for i in range(3)])
                b16r = Ring([sa("b16r%d" % i, [128, 256], BF16) for i in range(3)])
                sqr = Ring([sa("sqr%d" % i, [128, 256], F32) for i in range(2)])
                nrm = Ring([sa("nrm%d" % i, [128, 2], F32) for i in range(4)])
                ckf = sa("ckf", [128, D], F32)
                ckb = sa("ckb", [128, D], BF16)
                cksq = L["junk"]

                def qk_block(pp, b, j, TTc, dstT, mx, scale, out_rows, valid):
                    if out_rows is not None:
                        f = f32r.get()
                        S.copy("act", f, pp[:, 0:256])
                        if valid < 128:
                            S.dma("act", Tile(out_rows[0].ap[0:valid, b * 256:(b + 1) * 256], out_rows[0].buf), f[0:valid, :])
                        else:
                            S.dma("act", Tile(out_rows[0].ap[:, b * 256:(b + 1) * 256], out_rows[0].buf), f)
                    bb = b16r.get()
                    if out_rows is not None:
                        S.copy("dve", bb, f)
                    elif scale != 1.0:
                        S.ts("dve", bb, pp[:, 0:256], scale, ALU.mult)
                    else:
                        S.copy("dve", bb, pp[:, 0:256])
                    sq = sqr.get()
                    S.tt("pool", sq, bb, bb, ALU.mult)
                    nn = nrm.get()
                    S.op("dve", lambda e: e.reduce_sum(out=nn.ap, in_=sq.ap.rearrange("p (n d) -> p n d", n=2), axis=AX.X),
                         [sq], [nn])
                    S.tt("dve", mx[:, 2 * b:2 * b + 2], mx[:, 2 * b:2 * b + 2], nn, ALU.max)

                    def later():
                        pt = L["ptr_ring"].get()
                        for n in range(2):
                            S.tr(pt[:, n, :], bb[:, n * 128:(n + 1) * 128], self.ident, last=(n == 1))
                        self.evac(dstT[:, 2 * b:2 * b + 2, j * 128:(j + 1) * 128], pt[:, 0:2, :])
                    pendA.append(later)

                pendA = []

                def flushA(keep=0):
                    while len(pendA) > keep:
                        pendA.pop(0)()

                def phaseA_tile(x_src, r0, TTc, NSc, valid, krows, vrows):
                    if valid < 128:
                        S.copy("pool", xt_all[:, 0, :], self.xs_t)
                    else:
                        for j in range(NSc):
                            S.dma("sp", xt_all[:, j, :], x_src(j))
                    self.stage1(L, xt_all, hT, TTc, NSc)
                    for b in range(8):
                        wt = self.wload(L, w_in, kin, b * 256)
                        for j in range(NSc):
                            pp = self.tm_proj(L, wt, hT, j)
                            flushA(1)
                            qk_block(pp, b, j, TTc, qT_t, qmax, 128.0 ** -0.5, None, valid)
                    for b in range(8):
                        wt = self.wload(L, w_in, kin, D + b * 256)
                        for j in range(NSc):
                            pp = self.tm_proj(L, wt, hT, j)
                            flushA(1)
                            qk_block(pp, b, j, TTc, kT_t, kmax, 1.0, (krows(j),), valid)
                    for b in range(8):
                        wt = self.wload(L, w_in, kin, 2 * D + b * 256)
                        for j in range(NSc):
                            pp = self.tm_proj(L, wt, hT, j)
                            flushA()
                            f = f32r.get()
                            S.copy("act", f, pp[:, 0:256])
                            vr = vrows(j)
                            if valid < 128:
                                S.dma("act", Tile(vr.ap[0:valid, b * 256:(b + 1) * 256], vr.buf), f[0:valid, :])
                            else:
                                S.dma("act", Tile(vr.ap[:, b * 256:(b + 1) * 256], vr.buf), f)
                            S.copy("dve", v_t[:, j, b * 256:(b + 1) * 256], f)
                    for b in range(8):
                        wt = self.wload(L, w_in, kin, 3 * D + b * 256)
                        for j in range(NSc):
                            pp = self.tm_proj(L, wt, hT, j)
                            S.act(z_t[:, j, b * 256:(b + 1) * 256], pp[:, 0:256], AF.Silu)
                    S.dma("act", Tile(qT_d[:, :, r0:r0 + TTc].rearrange("n d t -> d n t"), tb("qT_d", r0)), qT_t[:, :, 0:TTc])
                    S.dma("act", Tile(kT_d[:, :, r0:r0 + TTc].rearrange("n d t -> d n t"), tb("kT_d", r0)), kT_t[:, :, 0:TTc])
                    S.dma("act", Tile(v_d[r0:r0 + TTc, :].rearrange("(j p) c -> p j c", p=128), tb("v_d", r0)), v_t[:, 0:NSc, :])
                    S.dma("act", Tile(z_d[r0:r0 + TTc, :].rearrange("(j p) c -> p j c", p=128), tb("z_d", r0)), z_t[:, 0:NSc, :])

                xsrc_ap = io["xp"] if first else io["xres"]
                xsrc_key = "xp" if first else "xres"
                for ti in range(ntile):
                    r0 = ti * TT
                    phaseA_tile(lambda j: Tile(xsrc_ap[r0 + j * 128:r0 + (j + 1) * 128, :], self.dbuf((xsrc_key, r0 + j * 128))),
                                r0, TT, NS, 128,
                                lambda j: Tile(io["krp"][r0 + j * 128:r0 + (j + 1) * 128, :], self.dbuf(("krp", r0 + j * 128))),
                                lambda j: Tile(io["vrp"][r0 + j * 128:r0 + (j + 1) * 128, :], self.dbuf(("vrp", r0 + j * 128))))
                    if ti == 0:
                        self.prep_next(li)
                    self.prep_some(4)
                self.prep_some(1000)
                if self.with_sample:
                    phaseA_tile(None, T, 128, 1, DEC,
                                lambda j: Tile(io["krs"], self.dbuf("krs")),
                                lambda j: Tile(io["vrs"], self.dbuf("vrs")))
                    for kt in range(PAST // 128):
                        S.dma("sp", ckf, Tile(io["ck"][kt * 128:(kt + 1) * 128, :], self.dbuf("ck")))
                        S.copy("act", ckb, ckf)
                        S.tt("pool", ckf, ckf, ckf, ALU.mult)
                        nn = sa("cnn%d" % kt, [128, 16], F32)
                        S.op("dve", lambda e, nn=nn: e.reduce_sum(out=nn.ap, in_=ckf.ap.rearrange("p (n d) -> p n d", n=16), axis=AX.X),
                             [ckf], [nn])
                        S.tt("dve", kmax, kmax, nn, ALU.max)
                        for q4 in range(4):
                            pt = L["ptr_ring"].get()
                            for i in range(4):
                                n = 4 * q4 + i
                                S.tr(pt[:, i, :], ckb[:, n * 128:(n + 1) * 128], self.ident, last=(i == 3))
                            self.evac(kTc[:, 4 * q4:4 * q4 + 4, kt * 128:(kt + 1) * 128], pt[:, 0:4, :])
                        S.dma("sp", ckf, Tile(io["cv"][kt * 128:(kt + 1) * 128, :], self.dbuf("cv")))
                        S.copy("dve", vc[:, kt, :, 0:256], ckf.re("p (h c) -> p h c", h=8))
                pp = L["pproj_ring"].get()
                S.rmax(stat[:, 8:9], qmax)
                S.rmax(stat[:, 9:10], kmax)
                S.tr(pp[0:2, 0:128], stat[:, 8:10], self.ident_f)
                S.copy("dve", stat[0:2, 16:48].bc([2, 32]) if False else stat[0:2, 10:11], stat[0:2, 10:11]) if False else None
                mx2 = sa("mx2", [2, 128], F32)
                S.copy("dve", mx2, pp[0:2, 0:128])
                S.rmax(stat[0:2, 10:11], mx2)
                S.act(stat[0:2, 11:12], stat[0:2, 10:11], AF.Ln)
                pp2 = L["pproj_ring"].get()
                S.mm(pp2[:, 0:1], self.ones_f[0:2, 0:128], stat[0:2, 11:12], start=True, stop=True)
                S.act(stat[:, 12:13], pp2[:, 0:1], AF.Exp, scale=0.5)
                S.ts("dve", negc, stat[:, 12:13], -1.0, ALU.mult)
            S.barrier()

            if DBG:
                print("phase A end n_inst", S.n_inst, flush=True)
            def attend(ebs, qT_h, kT_h, v_of_kt, nkt_of_qt, n_qt, mask_kind, ps_ring, po_ring, p_ring, fin, o_dst):
                pend = []

                def flush(keep):
                    while len(pend) > keep:
                        pend.pop(0)()

                for qt in range(n_qt):
                    po1 = po_ring.get()
                    po2 = po_ring.get()
                    nkt = nkt_of_qt(qt)
                    for kt in range(nkt):
                        ps = ps_ring.get()
                        for n in range(2):
                            S.mm(ps[:, n * 128:(n + 1) * 128], kT_h[:, n, kt * 128:(kt + 1) * 128],
                                 qT_h[:, n, qt * 128:(qt + 1) * 128], start=True, stop=True)
                        p = p_ring.get()
                        S.act(p, ps[:, 0:256], AF.Exp, bias=negc)
                        if kt == nkt - 1:
                            if mask_kind == "causal":
                                S.memset("pool", p.re("p (n t) -> p n t", n=2)[64:128, :, 0:64], 0.0)
                            else:
                                S.memset("pool", p[32:64, :], 0.0)
                                S.memset("pool", p[64:128, :], 0.0)
                        flush(3)

                        def pv(p=p, kt=kt, nkt=nkt, po1=po1, po2=po2):
                            S.mm(po1[:, 0:257], p[:, 0:128], v_of_kt(kt), start=(kt == 0), stop=(kt == nkt - 1))
                            S.mm(po2[:, 0:257], p[:, 128:256], v_of_kt(kt), start=(kt == 0), stop=(kt == nkt - 1))
                        pend.append(pv)

                    def finalize(qt=qt, po1=po1, po2=po2):
                        F = fin.get()
                        S.recip(F["r"][:, 0:1], po1[:, 256:257])
                        S.ts("dve", F["on"], po1[:, 0:256], F["r"][:, 0:1], ALU.mult)
                        S.recip(F["r"][:, 1:2], po2[:, 256:257])
                        S.tt("dve", F["r"][:, 1:2], F["r"][:, 1:2], neglam, ALU.mult)
                        S.stt("dve", F["on"], po2[:, 0:256], F["r"][:, 1:2], F["on"], ALU.mult, ALU.add)
                        S.act(F["sq"], F["on"], AF.Square, accum=F["r"][:, 2:3])
                        S.act(F["r"][:, 3:4], F["r"][:, 2:3], AF.Ln, bias=self.eps_t, scale=1.0 / 256)
                        S.act(F["r"][:, 4:5], F["r"][:, 3:4], AF.Exp, scale=-0.5)
                        S.stt("dve", F["ob"], F["on"], F["r"][:, 4:5], slw_b, ALU.mult, ALU.mult)
                        S.dma("pool", o_dst(qt), F["ob"])
                    pend.append(finalize)
                flush(0)

            with ExitStack() as eb:
                sbb = lambda n, s, d: self.sb(eb, "l%dB_%s" % (li, n), s, d)
                psb_ = lambda n, s, d: self.ps(eb, "l%dB_%s" % (li, n), s, d)
                ps_ring = Ring([psb_("ps%d" % i, [128, 512], F32) for i in range(4)])
                po_ring = Ring([psb_("po%d" % i, [128, 512], F32) for i in range(4)])
                p_ring = Ring([sbb("p%d" % i, [128, 256], BF16) for i in range(6)])
                fin = Ring([dict(r=sbb("fr%d" % i, [128, 8], F32), on=sbb("fon%d" % i, [128, 256], F32),
                                 sq=sbb("fsq%d" % i, [128, 256], BF16), ob=sbb("fob%d" % i, [128, 256], BF16))
                            for i in range(2)])
                hb = []
                for i in range(2):
                    hb.append(dict(q=sbb("qTh%d" % i, [128, 2, T], BF16), k=sbb("kTh%d" % i, [128, 2, T], BF16),
                                   v=sbb("vh%d" % i, [128, NKT, 257], BF16)))
                    S.memset("pool", hb[i]["v"][:, :, 256:257], 1.0)
                for h in range(8):
                    H = hb[h % 2]
                    S.dma("sp", H["q"], Tile(qT_d[2 * h:2 * h + 2, :, 0:T].rearrange("n d t -> d n t"), self.dbuf(("qT_d", 0))),
                          extra_reads=all_tb("qT_d"))
                    S.dma("sp", H["k"], Tile(kT_d[2 * h:2 * h + 2, :, 0:T].rearrange("n d t -> d n t"), self.dbuf(("kT_d", 0))),
                          extra_reads=all_tb("kT_d"))
                    for k0 in range(0, NKT, 8):
                        k1 = min(NKT, k0 + 8)
                        S.dma("sp", H["v"][:, k0:k1, 0:256],
                              Tile(v_d[k0 * 128:k1 * 128, h * 256:(h + 1) * 256].rearrange("(kt p) c -> p kt c", p=128),
                                   self.dbuf(("v_d", 0))), extra_reads=all_tb("v_d"))
                    attend(eb, H["q"], H["k"], lambda kt, H=H: H["v"][:, kt, :], lambda qt: qt + 1, NKT, "causal",
                           ps_ring, po_ring, p_ring, fin,
                           lambda qt, h=h: Tile(o_d[qt * 128:(qt + 1) * 128, h * 256:(h + 1) * 256],
                                                self.dbuf(("o_d", (qt * 128) // TT))))
                if self.with_sample:
                    qTs = sbb("qTs", [128, 16, 128], BF16)
                    S.dma("sp", qTs, Tile(qT_d[:, :, T:TP].rearrange("n d t -> d n t"), self.dbuf(("qT_d", ntile))))
                    S.dma("sp", kTc[:, :, PAST:PAST + 128], Tile(kT_d[:, :, T:TP].rearrange("n d t -> d n t"), self.dbuf(("kT_d", ntile))))
                    S.dma("sp", vc[:, PAST // 128, :, 0:256], Tile(v_d[T:TP, :].rearrange("p (h c) -> p h c", h=8), self.dbuf(("v_d", ntile))))
                    nk = PAST // 128 + 1
                    for h in range(8):
                        attend(eb, qTs[:, 2 * h:2 * h + 2, :], kTc[:, 2 * h:2 * h + 2, :], lambda kt, h=h: vc[:, kt, h, :],
                               lambda qt: nk, 1, "pad", ps_ring, po_ring, p_ring, fin,
                               lambda qt, h=h: Tile(o_d[T:TP, h * 256:(h + 1) * 256], self.dbuf(("o_d", ntile))))
            S.barrier()

            if DBG:
                print("phase B end n_inst", S.n_inst, flush=True)
            with ExitStack() as ec:
                L = {}
                sc = lambda n, s, d: self.sb(ec, "l%dC_%s" % (li, n), s, d)
                psc = lambda n, s, d: self.ps(ec, "l%dC_%s" % (li, n), s, d)
                xt_all = sc("xt", [128, NS, D], F32)
                gT = sc("gT", [128, KC, TT], BF16)
                o_t = sc("o_t", [128, NS, D], BF16)
                z_t = sc("z_t", [128, NS, D], BF16)
                L["junk"] = sc("junk", [128, D], BF16)
                L["ss_ring"] = Ring([sc("ss%d" % i, [128, 4], F32) for i in range(4)])
                L["wring"] = Ring([sc("w%d" % i, [128, KC, 256], BF16) for i in range(3)])
                L["pproj_ring"] = Ring([psc("pp%d" % i, [128, 512], F32) for i in range(3)])
                L["ptr_ring"] = Ring([psc("pt%d" % i, [128, 8, 128], BF16) for i in range(2)])
                fnw_b = None
                if last:
                    fnw_b = sc("fnw_b", [128, D], F32)
                    S.dma("sp", fnw_b, Tile(io["final_norm_w"].partition_broadcast(128), self.dbuf("final_norm_w")))

                def phaseC_tile(x_src, x_dst, y_dst, r0, TTc, NSc, valid):
                    if valid < 128:
                        S.copy("pool", xt_all[:, 0, :], self.xs_t)
                    else:
                        for j in range(NSc):
                            S.dma("sp", xt_all[:, j, :], x_src(j))
                    S.dma("sp", o_t[:, 0:NSc, :], Tile(o_d[r0:r0 + TTc, :].rearrange("(j p) c -> p j c", p=128), tb("o_d", r0)))
                    S.dma("sp", z_t[:, 0:NSc, :], Tile(z_d[r0:r0 + TTc, :].rearrange("(j p) c -> p j c", p=128), tb("z_d", r0)))
                    for j in range(NSc):
                        S.tt("pool", o_t[:, j, :], o_t[:, j, :], z_t[:, j, :], ALU.mult)
                        for q4 in range(4):
                            pt = L["ptr_ring"].get()
                            for i in range(4):
                                c = 4 * q4 + i
                                S.tr(pt[:, i, :], o_t[:, j, c * 128:(c + 1) * 128], self.ident, last=(i == 3))
                            self.evac(gT[:, 4 * q4:4 * q4 + 4, j * 128:(j + 1) * 128], pt[:, 0:4, :])
                    for cb in range(8):
                        wt = self.wload(L, w_out, kout, cb * 256)
                        cols = slice(cb * 256, (cb + 1) * 256)
                        for j in range(NSc):
                            pp = L["pproj_ring"].get()
                            for ic in range(KC):
                                S.mm(pp[:, 0:256], gT[:, ic, j * 128:(j + 1) * 128], wt[:, ic, :],
                                     start=(ic == 0), stop=(ic == KC - 1))
                            S.tt("dve", xt_all[:, j, cols], xt_all[:, j, cols], pp[:, 0:256], ALU.add)
                    for j in range(NSc):
                        if valid < 128:
                            S.copy("pool", self.xs_t, xt_all[:, 0, :])
                        elif not last:
                            S.dma("sp", x_dst(j), xt_all[:, j, :])
                        if last:
                            self.final_norm(L, xt_all[:, j, :], fnw_b, y_dst(j), valid)

                xsrc_ap = io["xp"] if first else io["xres"]
                xsrc_key = "xp" if first else "xres"
                for ti in range(ntile):
                    r0 = ti * TT
                    phaseC_tile(lambda j: Tile(xsrc_ap[r0 + j * 128:r0 + (j + 1) * 128, :], self.dbuf((xsrc_key, r0 + j * 128))),
                                lambda j: Tile(io["xres"][r0 + j * 128:r0 + (j + 1) * 128, :], self.dbuf(("xres", r0 + j * 128))),
                                lambda j: Tile(io["yp"][r0 + j * 128:r0 + (j + 1) * 128, :], self.dbuf(("yp", r0 + j * 128))),
                                r0, TT, NS, 128)
                if self.with_sample:
                    phaseC_tile(None, None, lambda j: Tile(io["ys"], self.dbuf("ys")), T, 128, 1, DEC)
            S.barrier()


def make_consts():
    c = np.zeros((128, 512), np.float32)
    c[:, 0:128] = np.eye(128, dtype=np.float32)
    s = np.arange(128)
    c[:, 128:256] = (s[:, None] <= s[None, :]).astype(np.float32)
    c[:, 256:384] = 1.0
    return c


def make_in_maps(inputs, n_cores=8, T=4096):
    consts = make_consts()
    maps = []
    f = lambda a: np.ascontiguousarray(np.asarray(a, dtype=np.float32))
    shared = {k: f(inputs[k]) for k in
              ["norm_w", "final_norm_w", "gla_w_in", "gla_w_a1", "gla_w_a2", "gla_b_a", "gla_norm_w",
               "gla_w_out", "hgrn_w_in", "hgrn_lower_bounds", "hgrn_norm_w", "hgrn_w_out",
               "diff_w_in", "diff_lambda", "diff_subln_w", "diff_w_out"]}
    xp = f(inputs["x_prompt"])
    xs = f(inputs["x_sample"])
    sg = f(inputs["state_gla"])
    sh = f(inputs["state_hgrn"])
    ck = f(inputs["cache_k"])
    cv = f(inputs["cache_v"])
    nb = xp.shape[0]
    for c in range(n_cores):
        m = dict(shared)
        m["xp"] = np.ascontiguousarray(xp[c % nb, :T])
        m["xs"] = np.ascontiguousarray(xs[c])
        m["sg"] = np.ascontiguousarray(sg[:, c])
        m["sh"] = np.ascontiguousarray(sh[:, c])
        m["ck"] = np.ascontiguousarray(ck[0, c].reshape(PAST, D))
        m["cv"] = np.ascontiguousarray(cv[0, c].reshape(PAST, D))
        m["consts"] = consts
        maps.append(m)
    return maps


_PROG_CACHE = {}


def kernel(**inputs):
    T = 4096
    if "full" not in _PROG_CACHE:
        _PROG_CACHE["full"] = Prog(T=T).build()
    nc = _PROG_CACHE["full"]
    maps = make_in_maps(inputs, 8, T)
    res = run_bass_kernel_spmd(nc, maps, core_ids=list(range(8)))
    R = res.results
    B = 4
    y_prompt = np.stack([R[b]["yp"] for b in range(B)])
    y_sample = np.stack([R[c]["ys"] for c in range(8)])
    gsp = np.stack([R[b]["gsp"] for b in range(B)], axis=1)
    gss = np.stack([R[c]["gss"] for c in range(8)], axis=1)
    hsp = np.stack([R[b]["hsp"] for b in range(B)], axis=1)
    hss = np.stack([R[c]["hss"] for c in range(8)], axis=1)
    krp = np.stack([R[b]["krp"] for b in range(B)])[None].reshape(1, B, T, 16, 128)
    vrp = np.stack([R[b]["vrp"] for b in range(B)])[None].reshape(1, B, T, 8, 256)
    krs = np.stack([R[c]["krs"] for c in range(8)])[None].reshape(1, 8, DEC, 16, 128)
    vrs = np.stack([R[c]["vrs"] for c in range(8)])[None].reshape(1, 8, DEC, 8, 256)
    return (y_prompt, y_sample, gsp, gss, hsp, hss, krp, vrp, krs, vrs)
```
